# Optimizing a Trainium2 kernel written in Bass

```python
import jax, jax.numpy as jnp
from jax import lax
import numpy as np

D_MODEL = 1024
BATCH = 8
SEQ = 4096
DEPTH = 4

POOL_WIDTH = D_MODEL // 2
POOL_WINDOWS = (2, 4, 8, 16)
N_POOL_GROUPS = 4
POOL_GROUP = POOL_WIDTH // N_POOL_GROUPS
POOL_MAX_WINDOW = 16
CONV_WIDTH = D_MODEL // 2
CONV_KERNEL = 31
LRU_WIDTH = D_MODEL
LRU_HEADS = 4
LRU_BLOCK = LRU_WIDTH // LRU_HEADS
LRU_CONV = 4
LRU_C = 8.0
N_BRANCHES = 3
IN_WIDTH = 2 * POOL_WIDTH + 3 * CONV_WIDTH + 2 * LRU_WIDTH + N_BRANCHES * D_MODEL
EPS = 1e-6

kernel_name = "hybrid_pool_conv_rglru_gated_trunk"


def rmsnorm(x, g):
    xf = x.astype(jnp.float32)
    y = xf * lax.rsqrt(jnp.mean(xf * xf, axis=-1, keepdims=True) + EPS)
    return (y * g.astype(jnp.float32)).astype(x.dtype)


def layernorm(x, g, b):
    xf = x.astype(jnp.float32)
    mu = jnp.mean(xf, axis=-1, keepdims=True)
    var = jnp.mean(jnp.square(xf - mu), axis=-1, keepdims=True)
    y = (xf - mu) * lax.rsqrt(var + EPS)
    return (y * g.astype(jnp.float32) + b.astype(jnp.float32)).astype(x.dtype)


def causal_depthwise_conv(x, w, b):
    k = w.shape[0]
    c = x.shape[-1]
    y = lax.conv_general_dilated(
        x, w[:, None, :].astype(x.dtype), window_strides=(1,), padding=[(k - 1, 0)],
        dimension_numbers=("NWC", "WIO", "NWC"), feature_group_count=c)
    return y + b.astype(x.dtype)


def pool_mixer(u, w_grp, scale):
    bsz, s, _ = u.shape
    uf = u.astype(jnp.float32).reshape(bsz, s, N_POOL_GROUPS, POOL_GROUP)
    csum = jnp.cumsum(uf, axis=1)
    cpad = jnp.pad(csum, ((0, 0), (POOL_MAX_WINDOW, 0), (0, 0), (0, 0)))
    t = jnp.arange(s)
    means = []
    for g, w in enumerate(POOL_WINDOWS):
        lag = lax.slice_in_dim(cpad, POOL_MAX_WINDOW - w, POOL_MAX_WINDOW - w + s, axis=1)[:, :, g]
        cnt = jnp.minimum(t + 1, w).astype(jnp.float32)[None, :, None]
        means.append((csum[:, :, g] - lag) / cnt)
    pooled = (jnp.stack(means, axis=2) - uf).astype(u.dtype)
    y = jnp.einsum("bsgc,gcd->bsgd", pooled, w_grp)
    return y.reshape(bsz, s, POOL_WIDTH) * scale


def rg_lru(x, w_a, b_a, w_x, b_x, lam):
    bsz, s, _ = x.shape
    xh = x.reshape(bsz, s, LRU_HEADS, LRU_BLOCK)
    r = jax.nn.sigmoid(jnp.einsum("bshi,hij->bshj", xh, w_a).reshape(bsz, s, LRU_WIDTH) + b_a)
    i = jax.nn.sigmoid(jnp.einsum("bshi,hij->bshj", xh, w_x).reshape(bsz, s, LRU_WIDTH) + b_x)
    log_a = -LRU_C * r.astype(jnp.float32) * jax.nn.softplus(-lam.astype(jnp.float32))
    a = jnp.exp(log_a)
    mult = jnp.sqrt(-jnp.expm1(2.0 * log_a))
    bterm = mult * (i * x).astype(jnp.float32)

    def combine(left, right):
        a1, b1 = left
        a2, b2 = right
        return a1 * a2, a2 * b1 + b2

    _, h = lax.associative_scan(combine, (a, bterm), axis=1)
    return h.astype(x.dtype)


def hybrid_layer(x, norm_pre, w_in, pool_w, pool_scale, w_pool_out, conv_dw, conv_b,
                 conv_ln_g, conv_ln_b, w_conv_out, lru_conv_w, lru_conv_b, lru_wa, lru_ba,
                 lru_wx, lru_bx, lru_lambda, w_lru_out, w_out, norm_post):
    h = rmsnorm(x, norm_pre)
    z = h @ w_in
    o = 0
    def take(n):
        nonlocal o
        piece = z[..., o:o + n]
        o += n
        return piece
    p_val, p_gate = take(POOL_WIDTH), take(POOL_WIDTH)
    c_val, c_glu, c_gate = take(CONV_WIDTH), take(CONV_WIDTH), take(CONV_WIDTH)
    r_val, r_gate = take(LRU_WIDTH), take(LRU_WIDTH)
    g_pool, g_conv, g_lru = take(D_MODEL), take(D_MODEL), take(D_MODEL)

    y_pool = (pool_mixer(p_val, pool_w, pool_scale) * jax.nn.silu(p_gate)) @ w_pool_out

    c = c_val * jax.nn.sigmoid(c_glu)
    c = causal_depthwise_conv(c, conv_dw, conv_b)
    c = jax.nn.silu(layernorm(c, conv_ln_g, conv_ln_b))
    y_conv = (c * jax.nn.silu(c_gate)) @ w_conv_out

    r = causal_depthwise_conv(r_val, lru_conv_w, lru_conv_b)
    r = rg_lru(r, lru_wa, lru_ba, lru_wx, lru_bx, lru_lambda)
    y_lru = (r * jax.nn.silu(r_gate)) @ w_lru_out

    merged = (jax.nn.sigmoid(g_pool) * y_pool + jax.nn.sigmoid(g_conv) * y_conv
              + jax.nn.sigmoid(g_lru) * y_lru)
    out = merged @ w_out
    return x + rmsnorm(out, norm_post)


def setup_inputs(seed: int = 0) -> dict:
    key = jax.random.key(seed)
    ks = jax.random.split(key, 24)
    f32 = jnp.float32
    def nrm(k, shape, fan_in):
        return jax.random.normal(k, shape, f32) * (fan_in ** -0.5)
    def small(k, shape, s=0.01):
        return jax.random.normal(k, shape, f32) * s
    a0 = jax.random.uniform(ks[17], (DEPTH, LRU_WIDTH), f32, 0.9, 0.999)
    sig = a0 ** (1.0 / LRU_C)
    lru_lambda = jnp.log(sig) - jnp.log1p(-sig)
    return {
        "x": jax.random.normal(ks[0], (BATCH, SEQ, D_MODEL), f32),
        "norm_pre": 1.0 + small(ks[1], (DEPTH, D_MODEL), 0.05),
        "w_in": nrm(ks[2], (DEPTH, D_MODEL, IN_WIDTH), D_MODEL),
        "pool_w": nrm(ks[3], (DEPTH, N_POOL_GROUPS, POOL_GROUP, POOL_GROUP), POOL_GROUP),
        "pool_scale": 1.0 + small(ks[4], (DEPTH, POOL_WIDTH), 0.05),
        "w_pool_out": nrm(ks[5], (DEPTH, POOL_WIDTH, D_MODEL), POOL_WIDTH),
        "conv_dw": nrm(ks[6], (DEPTH, CONV_KERNEL, CONV_WIDTH), CONV_KERNEL),
        "conv_b": small(ks[7], (DEPTH, CONV_WIDTH)),
        "conv_ln_g": 1.0 + small(ks[8], (DEPTH, CONV_WIDTH), 0.05),
        "conv_ln_b": small(ks[9], (DEPTH, CONV_WIDTH)),
        "w_conv_out": nrm(ks[10], (DEPTH, CONV_WIDTH, D_MODEL), CONV_WIDTH),
        "lru_conv_w": nrm(ks[11], (DEPTH, LRU_CONV, LRU_WIDTH), LRU_CONV),
        "lru_conv_b": small(ks[12], (DEPTH, LRU_WIDTH)),
        "lru_wa": nrm(ks[13], (DEPTH, LRU_HEADS, LRU_BLOCK, LRU_BLOCK), LRU_BLOCK),
        "lru_ba": small(ks[14], (DEPTH, LRU_WIDTH)),
        "lru_wx": nrm(ks[15], (DEPTH, LRU_HEADS, LRU_BLOCK, LRU_BLOCK), LRU_BLOCK),
        "lru_bx": small(ks[16], (DEPTH, LRU_WIDTH)),
        "lru_lambda": lru_lambda,
        "w_lru_out": nrm(ks[18], (DEPTH, LRU_WIDTH, D_MODEL), LRU_WIDTH),
        "w_out": nrm(ks[19], (DEPTH, D_MODEL, D_MODEL), D_MODEL),
        "norm_post": 1.0 + small(ks[20], (DEPTH, D_MODEL), 0.05),
    }


def reference(x, norm_pre, w_in, pool_w, pool_scale, w_pool_out, conv_dw, conv_b, conv_ln_g,
              conv_ln_b, w_conv_out, lru_conv_w, lru_conv_b, lru_wa, lru_ba, lru_wx, lru_bx,
              lru_lambda, w_lru_out, w_out, norm_post):
    for l in range(DEPTH):
        x = hybrid_layer(x, norm_pre[l], w_in[l], pool_w[l], pool_scale[l], w_pool_out[l],
                         conv_dw[l], conv_b[l], conv_ln_g[l], conv_ln_b[l], w_conv_out[l],
                         lru_conv_w[l], lru_conv_b[l], lru_wa[l], lru_ba[l], lru_wx[l],
                         lru_bx[l], lru_lambda[l], w_lru_out[l], w_out[l], norm_post[l])
    return x
```

```python
import numpy as np
import concourse.bass as bass
import concourse.mybir as mybir
from concourse.bass_utils import run_bass_kernel_spmd

F32 = mybir.dt.float32
F32R = mybir.dt.float32r
BF16 = mybir.dt.bfloat16
AF = mybir.ActivationFunctionType
ALU = mybir.AluOpType

D = 1024
SEQ = 4096
DEPTH = 4
T = 512
NKT = 8
INW = 7680
EPS = 1e-6
NVL = 220
NCONST = 1792
CH = 4096
RING = 5
NF32 = 16
NSTG = 3
NBF = 14
SAME_ENGINE_SYNC = True

NPK = 23
PK_USED = [CH] * 15 + [512, CH, CH, CH, CH, CH, CH, CH]
NSC = 28
SC_USED = PK_USED + [31 * 128] * 4 + [CH]


class Op:
    __slots__ = ("eng", "fn", "deps", "sig", "val", "dma_key", "uid")

    def __init__(self, eng, fn, dma_key=None, uid=0):
        self.eng = eng
        self.fn = fn
        self.deps = []
        self.sig = dma_key is not None
        self.val = 0
        self.dma_key = dma_key
        self.uid = uid


class Prog:
    ENGS = ("pe", "act", "dve", "pool", "sp")

    def __init__(self):
        self.ops = {e: [] for e in self.ENGS}
        self.bufs = {}
        self.n = 0
        self.dma_cnt = {}

    def _dep(self, op, prod, raw):
        if prod is op:
            return
        if prod.dma_key is None and prod.eng == op.eng:
            if op.eng == "pe" or not raw or not SAME_ENGINE_SYNC:
                return
        prod.sig = True
        if prod not in op.deps:
            op.deps.append(prod)

    def add(self, eng, fn, reads=(), writes=(), dma_key=None):
        self.n += 1
        op = Op(eng, fn, dma_key, self.n)
        for k in reads:
            b = self.bufs.setdefault(k, {"w": {}, "r": {}})
            for w in b["w"].values():
                self._dep(op, w, True)
        for k in writes:
            b = self.bufs.setdefault(k, {"w": {}, "r": {}})
            for w in b["w"].values():
                self._dep(op, w, False)
            for r in b["r"].values():
                self._dep(op, r, False)
        rk = eng if dma_key is None else ("dma", op.uid)
        for k in reads:
            self.bufs[k]["r"][rk] = op
        for k in writes:
            b = self.bufs[k]
            b["w"] = {rk: op}
            b["r"] = {}
        if dma_key is not None:
            self.dma_cnt[dma_key] = self.dma_cnt.get(dma_key, 0) + 16
            op.val = self.dma_cnt[dma_key]
        self.ops[eng].append(op)
        return op

    def emit(self, nc, sems, final_waits):
        for e in self.ENGS:
            c = 0
            for op in self.ops[e]:
                if op.dma_key is None and op.sig:
                    c += 1
                    op.val = c
        engs = {"pe": "tensor", "act": "scalar", "dve": "vector", "pool": "gpsimd", "sp": "sync"}

        def run(ename):
            def body(eng):
                waited = {}
                for op in self.ops[ename]:
                    for d in op.deps:
                        key = d.dma_key if d.dma_key is not None else d.eng
                        if waited.get(key, 0) >= d.val:
                            continue
                        eng.wait_ge(sems[key], d.val)
                        waited[key] = d.val
                    ins = op.fn(eng)
                    if op.dma_key is not None:
                        ins.then_inc(sems[op.dma_key], 16)
                    elif op.sig:
                        ins.then_inc(sems[ename], 1)
                if ename == "sp":
                    for key in final_waits:
                        eng.wait_ge(sems[key], self.dma_cnt[key])
            return body

        with nc.Block() as block:
            for ename in self.ENGS:
                if self.ops[ename]:
                    getattr(block, engs[ename])(run(ename))


def build_nc(n_tiles=SEQ // T, depth=DEPTH, debug=False):
    ntok = n_tiles * T
    nc = bass.Bass("TRN2", target_bir_lowering=False, dynamic_dma_scratch_size=1024)
    xT = nc.dram_tensor("xT", [NKT, 128, ntok], F32, kind="ExternalInput").ap()
    wpk = nc.dram_tensor("wpk", [depth, NPK, 128, CH], F32, kind="ExternalInput").ap()
    vecs_d = nc.dram_tensor("vecs", [128, DEPTH * NVL], F32, kind="ExternalInput").ap()
    cst_d = nc.dram_tensor("cst", [128, NCONST], F32, kind="ExternalInput").ap()
    oT = nc.dram_tensor("oT", [NKT, 128, ntok], F32, kind="ExternalOutput").ap()
    wsc = nc.dram_tensor("wsc", [depth, NSC, 128, CH], BF16, kind="Internal").ap()

    P = Prog()
    dbg = {}

    def dump(name, ap, keys, dt):
        if not debug:
            return
        shp = list(ap.shape)
        d_ = nc.dram_tensor("dbg_" + name, shp, dt, kind="ExternalOutput").ap()
        dbg[name] = d_
        P.add("sp", lambda e: e.dma_start(out=d_, in_=ap), keys, [], dma_key="dbg")
    import contextlib
    es = contextlib.ExitStack()
    with es:
        def sb(name, shape, dt):
            return es.enter_context(nc.sbuf_tensor(name, shape, dt))

        X = sb("X", [128, NKT, T], F32)
        H = sb("H", [128, NKT, T], BF16)
        UP = sb("UP", [128, 4, T], BF16)
        UC = sb("UC", [128, 4, T], BF16)
        UL = sb("UL", [128, 8, T], BF16)
        MB = sb("MB", [128, 8, T], BF16)
        M = sb("M", [128, 8, T], F32)
        RSTD = sb("RSTD", [128, T], F32)
        PTOKS = sb("PTOKS", [128, DEPTH, T], BF16)
        CHS = sb("CHS", [128, DEPTH, 4, 32], BF16)
        RHS = sb("RHS", [128, DEPTH, 8, 4], BF16)
        HST = sb("HST", [128, DEPTH, 8], F32)
        C = sb("C", [128, 4, T + 32], BF16)
        R = sb("R", [128, 8, T + 4], BF16)
        FP = sb("FP", [128, NF32, T], F32)
        BP = sb("BP", [128, NBF, T], BF16)
        RNG = sb("RNG", [128, RING, CH], BF16)
        VEC = sb("VEC", [128, DEPTH * NVL], F32)
        DV = sb("DV", [128, DEPTH * 40], F32)
        TMPV = sb("TMPV", [128, 4 * 32], F32)
        CST = sb("CST", [128, 128], F32)
        ONES = sb("ONES", [128, 128], F32R)
        EPSD = sb("EPSD", [128, 4], F32)
        SQR = sb("SQR", [128, 8, T], F32R)
        STG = sb("STG", [128, NSTG, CH // 2], F32)
        CF = sb("CF", [128, 2, T], F32)
        BAND = sb("BAND", [128, 12, 128], BF16)
        PS = es.enter_context(nc.psum_tensor("PS", [128, 8, T], F32))

        sem_names = ["pe", "act", "dve", "pool", "sp", "x", "o", "vec", "cst", "sf0", "sf1", "sf2", "dbg"] + [f"wb{i}" for i in range(RING)] + \
                    [f"ring{i}" for i in range(RING)]
        sems = {k: es.enter_context(nc.semaphore(k)) for k in sem_names}

        fcnt = [0]
        ccnt = [0]
        bcnt = [0]
        pcnt = [0]

        def ftile():
            i = fcnt[0] % NF32
            fcnt[0] += 1
            return FP[:, i, :], f"FP{i}"

        def btile():
            i = bcnt[0] % NBF
            bcnt[0] += 1
            return BP[:, i, :], f"BP{i}"

        def pbank():
            i = pcnt[0] % 8
            pcnt[0] += 1
            return PS[:, i, :], f"PS{i}"

        def mm(out, lhsT, rhs, start, stop, reads, writes):
            return P.add("pe", lambda e: e.matmul(out, lhsT, rhs, start=start, stop=stop), reads, writes)

        def act(out, in_, func, reads, writes, bias=None, scale=None):
            kw = {}
            if bias is not None:
                kw["bias"] = bias
            if scale is not None:
                kw["scale"] = scale
            return P.add("act", lambda e: e.activation(out, in_, func, **kw), reads, writes)

        def tt(eng, out, in0, in1, op, reads, writes):
            return P.add(eng, lambda e: e.tensor_tensor(out, in0, in1, op), reads, writes)

        def ts(eng, out, in0, s1, op0, reads, writes, s2=None, op1=None):
            if op1 is None:
                return P.add(eng, lambda e: e.tensor_scalar(out, in0, s1, None, op0), reads, writes)
            return P.add(eng, lambda e: e.tensor_scalar(out, in0, s1, s2, op0, op1), reads, writes)

        def stt(out, in0, scalar, in1, op0, op1, reads, writes):
            return P.add("dve", lambda e: e.scalar_tensor_tensor(out, in0, scalar, in1, op0, op1), reads, writes)

        def cp(eng, out, in_, reads, writes):
            return P.add(eng, lambda e: e.tensor_copy(out, in_), reads, writes)

        def dma(out, in_, key, reads, writes, eng="sp"):
            return P.add(eng, lambda e: e.dma_start(out=out, in_=in_), reads, writes, dma_key=key)

        dma(VEC[:], vecs_d[:, :], "vec", [], ["VEC"])
        dma(CST[:], cst_d[:, 0:128], "cst", [], ["CST"])
        dma(STG[:, 0, 0:1536], cst_d[:, 256:1792], "sf0", [], ["STG0"])
        dma(STG[:, 1, 0:128], cst_d[:, 128:256], "sf1", [], ["STG1"])
        cp("dve", ONES[:], STG[:, 1, 0:128], ["STG1"], ["ONES"])
        cp("dve", BAND[:].rearrange("p a b -> p (a b)"), STG[:, 0, 0:1536], ["STG0"], ["BAND"])
        IDENT = CST[:, 0:128]
        P.add("pool", lambda e: e.memset(EPSD[:, 0:1], float(D * EPS)), [], ["EPSD"])
        P.add("pool", lambda e: e.memset(EPSD[:, 1:2], float(EPS)), [], ["EPSD"])
        P.add("pool", lambda e: e.memset(EPSD[:, 2:3], 1.0), [], ["EPSD"])

        for l in range(depth):
            vb = l * NVL
            db = l * 40
            ts("dve", DV[:, db:db + 8], VEC[:, vb:vb + 8], 32.0, ALU.mult, ["VEC"], ["DV"])
            ts("dve", DV[:, db + 8:db + 16], VEC[:, vb + 8:vb + 16], 32.0, ALU.mult, ["VEC"], ["DV"])
            tb = l * 32
            e_ = TMPV[:, tb:tb + 8]
            u_ = TMPV[:, tb + 8:tb + 16]
            d_ = TMPV[:, tb + 16:tb + 24]
            q_ = TMPV[:, tb + 24:tb + 32]
            act(e_, VEC[:, vb + 56:vb + 64], AF.Exp, ["VEC"], ["TMPV"], scale=-1.0)
            ts("dve", u_, e_, 1.0, ALU.add, ["TMPV"], ["TMPV"])
            ts("dve", d_, u_, -1.0, ALU.add, ["TMPV"], ["TMPV"], s2=1e-30, op1=ALU.max)
            P.add("dve", lambda e, d_=d_: e.reciprocal(d_, d_), ["TMPV"], ["TMPV"])
            tt("dve", q_, e_, d_, ALU.mult, ["TMPV"], ["TMPV"])
            act(u_, u_, AF.Ln, ["TMPV"], ["TMPV"])
            tt("dve", u_, u_, q_, ALU.mult, ["TMPV"], ["TMPV"])
            ts("dve", DV[:, db + 16:db + 24], u_, -4.0, ALU.mult, ["TMPV"], ["DV"])
            ts("dve", DV[:, db + 24:db + 32], VEC[:, vb + 40:vb + 48], 0.5, ALU.mult, ["VEC"], ["DV"])
            ts("dve", DV[:, db + 32:db + 40], VEC[:, vb + 48:vb + 56], 0.5, ALU.mult, ["VEC"], ["DV"])

        HC = CH // 2

        P.add("pool", lambda e: e.memset(CHS[:].rearrange("p a b c -> p (a b c)"), 0.0), [], ["CHS"])
        P.add("pool", lambda e: e.memset(RHS[:].rearrange("p a b c -> p (a b c)"), 0.0), [], ["RHS"])
        P.add("pool", lambda e: e.memset(HST[:].rearrange("p a b -> p (a b)"), 0.0), [], [f"HST{l}_{ft}" for l in range(DEPTH) for ft in range(8)])

        stream = []
        order = [0, 1, 15, 2, 3, 5, 6, 7, 8, 27, 16, 23, 24, 25, 26, 4, 9, 17, 10, 11, 18, 12, 13, 19, 14, 20, 21, 22]
        for it in range(n_tiles):
            for l in range(depth):
                for c in order:
                    stream.append((l, c))
        sstate = {"issued": 0, "next": 0}

        pending_wb = []
        cast_rot = [0]
        n_first = depth * len(order)

        def flush_wb():
            for (l, c, s, used) in pending_wb:
                dma(wsc[l, c, :, 0:used], RNG[:, s, 0:used], f"wb{s}", [f"RNG{s}"], [f"WSC{l}_{c}"], eng="act")
            pending_wb.clear()

        staged = {}

        def stage_in(i):
            if i in staged or i >= n_first:
                return
            l, c = stream[i]
            lst = []
            if c < NPK:
                used = SC_USED[c]
                for hh in range(2):
                    lo = hh * HC
                    if lo >= used:
                        continue
                    w = min(HC, used - lo)
                    q = cast_rot[0] % NSTG
                    cast_rot[0] += 1
                    dma(STG[:, q, 0:w], wpk[l, c, :, lo:lo + w], f"sf{q}", [], [f"STG{q}"])
                    lst.append((q, lo, w))
            staged[i] = lst

        def fill_from_fp32(i, l, c, s):
            used = SC_USED[c]
            vb = l * NVL
            stage_in(i)
            if c < NPK:
                for (q, lo, w) in staged[i]:
                    act(RNG[:, s, lo:lo + w], STG[:, q, 0:w], AF.Copy, [f"STG{q}"], [f"RNG{s}"])
            elif c < 27:
                ct = c - 23
                for k in range(31):
                    col = vb + 64 + ct * 31 + k
                    ts("dve", RNG[:, s, k * 128:(k + 1) * 128], IDENT, VEC[:, col:col + 1],
                       ALU.mult, ["CST", "VEC"], [f"RNG{s}"])
            else:
                for j in range(32):
                    col = vb + 188 + j
                    ts("dve", RNG[:, s, j * 128:(j + 1) * 128], IDENT, VEC[:, col:col + 1],
                       ALU.mult, ["CST", "VEC"], [f"RNG{s}"])
            pending_wb.append((l, c, s, used))
            nxt = i + 1
            while nxt < n_first and stream[nxt][1] >= NPK:
                nxt += 1
            stage_in(nxt)

        def issue_loads(upto):
            while sstate["issued"] < min(upto, len(stream)):
                i = sstate["issued"]
                l, c = stream[i]
                s = i % RING
                used = SC_USED[c]
                flush_wb()
                if i < n_first:
                    fill_from_fp32(i, l, c, s)
                else:
                    dma(RNG[:, s, 0:used], wsc[l, c, :, 0:used], f"ring{s}", [f"WSC{l}_{c}"], [f"RNG{s}"])
                sstate["issued"] += 1

        def next_chunk(expect, keep=1):
            i = sstate["next"]
            assert stream[i][1] == expect, (stream[i], expect)
            issue_loads(i + RING - keep)
            sstate["next"] += 1
            s = i % RING
            return RNG[:, s, :], f"RNG{s}"

        def sumsq_rstd(src_list, n_feat_scale_eps):
            ps, pk = pbank()
            for i, (a, k) in enumerate(src_list):
                mm(ps, ONES[:], a, i == 0, i == len(src_list) - 1, ["ONES", k], [pk])
            return ps, pk

        for it in range(n_tiles):
            tok0 = it * T
            dma(X[:], xT[:, :, tok0:tok0 + T].rearrange("k p t -> p k t"), "x", [], [f"X{k}" for k in range(NKT)])
            for l in range(depth):
                vb = l * NVL
                db = l * 40

                def vcol(off, i=0):
                    return VEC[:, vb + off + i:vb + off + i + 1]

                def dcol(off, i=0):
                    return DV[:, db + off + i:db + off + i + 1]

                sq = []
                for kt in range(NKT):
                    act(SQR[:, kt, :], X[:, kt, :], AF.Square, [f"X{kt}"], [f"SQR{kt}"])
                    sq.append((SQR[:, kt, :], f"SQR{kt}"))
                ps, pk = sumsq_rstd(sq, None)
                act(RSTD[:], ps, AF.Ln, [pk, "EPSD"], ["RSTD"], bias=EPSD[:, 0:1])
                act(RSTD[:], RSTD[:], AF.Exp, ["RSTD"], ["RSTD"], scale=-0.5)
                for kt in range(NKT):
                    stt(H[:, kt, :], X[:, kt, :], dcol(0, kt), RSTD[:], ALU.mult, ALU.mult,
                        [f"X{kt}", "DV", "RSTD"], [f"H{kt}"])
                Hk = [f"H{kt}" for kt in range(NKT)]
                DBG = debug and it == debug - 1 and l == 0
                if DBG:
                    dump("H", H[:], Hk, BF16)
                    dump("RSTD", RSTD[:], ["RSTD"], F32)

                w0, w0k = next_chunk(0)
                w0v = w0.rearrange("p (k f) -> p k f", k=NKT)
                ptok = []
                for t4 in range(4):
                    ps, pk = pbank()
                    for kt in range(NKT):
                        mm(ps, H[:, kt, t4 * 128:(t4 + 1) * 128], w0v[:, kt, :], kt == 0, kt == NKT - 1,
                           [Hk[kt], w0k], [pk])
                    b, bk = btile()
                    act(b, ps, AF.Copy, [pk], [bk])
                    ptok.append((b, bk))
                w1, w1k = next_chunk(1)
                w1v = w1.rearrange("p (k f) -> p k f", k=NKT)
                sg = []
                for ft in range(4):
                    ps, pk = pbank()
                    for kt in range(NKT):
                        mm(ps, w1v[:, kt, ft * 128:(ft + 1) * 128], H[:, kt, :], kt == 0, kt == NKT - 1,
                           [Hk[kt], w1k], [pk])
                    b, bk = btile()
                    act(b, ps, AF.Silu, [pk], [bk])
                    sg.append((b, bk))
                pw, pwk = next_chunk(15)
                pwv = pw[:, 0:512].rearrange("p (g f) -> p g f", g=4)
                for g in range(4):
                    ps, pk = pbank()
                    for t4 in range(4):
                        first = (it == 0 and t4 == 0)
                        bidx = (8 + g) if first else g
                        cur, curk = ptok[t4]
                        mm(ps[:, t4 * 128:(t4 + 1) * 128], cur[:, g * 128:(g + 1) * 128], BAND[:, bidx, :],
                           True, first, [curk, "BAND"], [pk])
                        if not first:
                            if t4 == 0:
                                prv, prvk = PTOKS[:, l, :], f"PTOKS{l}"
                            else:
                                prv, prvk = ptok[t4 - 1]
                            mm(ps[:, t4 * 128:(t4 + 1) * 128], prv[:, g * 128:(g + 1) * 128], BAND[:, 4 + g, :],
                               False, True, [prvk, "BAND"], [pk])
                    pl, plk = btile()
                    act(pl, ps, AF.Copy, [pk], [plk])
                    ps2, pk2 = pbank()
                    mm(ps2, pwv[:, g, :], pl, True, True, [plk, pwk], [pk2])
                    stt(UP[:, g, :], ps2, vcol(16, g), sg[g][0], ALU.mult, ALU.mult, [pk2, "VEC", sg[g][1]], [f"UP{g}"])
                cp("pool", PTOKS[:, l, :], ptok[3][0], [ptok[3][1]], [f"PTOKS{l}"])
                if DBG:
                    dump("UP", UP[:], [f"UP{g}" for g in range(4)], BF16)

                cp("pool", C[:, :, 0:30], CHS[:, l, :, 0:30], ["CHS"], [f"C{ft}" for ft in range(4)])
                w2, w2k = next_chunk(2)
                w3, w3k = next_chunk(3)
                w2v = w2.rearrange("p (k f) -> p k f", k=NKT)
                w3v = w3.rearrange("p (k f) -> p k f", k=NKT)
                for ft in range(4):
                    psv, pkv = pbank()
                    for kt in range(NKT):
                        mm(psv, w2v[:, kt, ft * 128:(ft + 1) * 128], H[:, kt, :], kt == 0, kt == NKT - 1,
                           [Hk[kt], w2k], [pkv])
                    psg, pkg = pbank()
                    for kt in range(NKT):
                        mm(psg, w3v[:, kt, ft * 128:(ft + 1) * 128], H[:, kt, :], kt == 0, kt == NKT - 1,
                           [Hk[kt], w3k], [pkg])
                    f, fk = M[:, ft, :], f"M{ft}"
                    act(f, psg, AF.Sigmoid, [pkg], [fk])
                    tt("dve", C[:, ft, 30:30 + T], psv, f, ALU.mult, [pkv, fk], [f"C{ft}"])
                cp("pool", CHS[:, l, :, 0:30], C[:, :, T:T + 30], [f"C{ft}" for ft in range(4)], ["CHS"])

                cc = []
                csq = []

                def conv31(ft, keep):
                    dd, ddk = next_chunk(23 + ft, keep=keep)
                    ddv = dd[:, 0:31 * 128].rearrange("p (k f) -> p k f", k=31)
                    ps, pk = pbank()
                    for k in range(31):
                        mm(ps, ddv[:, k, :], C[:, ft, k:k + T], k == 0, k == 30, [f"C{ft}", ddk], [pk])
                    a, ak = SQR[:, ft, :], f"SQR{ft}"
                    act(a, ps, AF.Identity, [pk, "VEC"], [ak], bias=vcol(20, ft))
                    a2, a2k = SQR[:, 4 + ft, :], f"SQR{4 + ft}"
                    act(a2, ps, AF.Square, [pk, "VEC"], [a2k], bias=vcol(20, ft))
                    cc.append((a, ak))
                    csq.append((a2, a2k))

                def conv_ln_tail():
                    w4, w4k = next_chunk(4)
                    w4v = w4.rearrange("p (k f) -> p k f", k=NKT)
                    scg = []
                    for ft in range(4):
                        ps, pk = pbank()
                        for kt in range(NKT):
                            mm(ps, w4v[:, kt, ft * 128:(ft + 1) * 128], H[:, kt, :], kt == 0, kt == NKT - 1,
                               [Hk[kt], w4k], [pk])
                        b, bk = btile()
                        act(b, ps, AF.Silu, [pk], [bk])
                        scg.append((b, bk))
                    psm, pkm = sumsq_rstd(cc, None)
                    pss, pks = sumsq_rstd(csq, None)
                    mean, meank = M[:, 4, :], "M4"
                    ts("dve", mean, psm, 1.0 / 512, ALU.mult, [pkm], [meank])
                    m2, m2k = M[:, 5, :], "M5"
                    tt("dve", m2, mean, mean, ALU.mult, [meank], [m2k])
                    var, vark = M[:, 6, :], "M6"
                    stt(var, pss, 1.0 / 512, m2, ALU.mult, ALU.subtract, [pks, m2k], [vark])
                    act(var, var, AF.Ln, [vark, "EPSD"], [vark], bias=EPSD[:, 1:2])
                    act(var, var, AF.Exp, [vark], [vark], scale=-0.5)
                    for ft in range(4):
                        a0, a0k = cc[ft]
                        a, ak = M[:, ft, :], f"M{ft}"
                        tt("dve", a, a0.bitcast(F32), mean, ALU.subtract, [a0k, meank], [ak])
                        tt("dve", a, a, var, ALU.mult, [ak, vark], [ak])
                        b, bk = btile()
                        act(b, a, AF.Silu, [ak, "VEC"], [bk], bias=vcol(28, ft), scale=vcol(24, ft))
                        tt("pool", UC[:, ft, :], b, scg[ft][0], ALU.mult, [bk, scg[ft][1]], [f"UC{ft}"])

                cp("pool", R[:, :, 0:3], RHS[:, l, :, 0:3], ["RHS"], [f"R{ft}" for ft in range(8)])
                for half in range(2):
                    w5, w5k = next_chunk(5 + half)
                    w5v = w5.rearrange("p (k f) -> p k f", k=NKT)
                    for f4 in range(4):
                        ft = half * 4 + f4
                        ps, pk = pbank()
                        for kt in range(NKT):
                            mm(ps, w5v[:, kt, f4 * 128:(f4 + 1) * 128], H[:, kt, :], kt == 0, kt == NKT - 1,
                               [Hk[kt], w5k], [pk])
                        act(R[:, ft, 3:3 + T], ps, AF.Copy, [pk], [f"R{ft}"])
                cp("pool", RHS[:, l, :, 0:3], R[:, :, T:T + 3], [f"R{ft}" for ft in range(8)], ["RHS"])
                srg = []
                for half in range(2):
                    w7, w7k = next_chunk(7 + half)
                    w7v = w7.rearrange("p (k f) -> p k f", k=NKT)
                    for f4 in range(4):
                        ps, pk = pbank()
                        for kt in range(NKT):
                            mm(ps, w7v[:, kt, f4 * 128:(f4 + 1) * 128], H[:, kt, :], kt == 0, kt == NKT - 1,
                               [Hk[kt], w7k], [pk])
                        ftg = half * 4 + f4
                        b, bk = UL[:, ftg, :], f"UL{ftg}"
                        act(b, ps, AF.Silu, [pk], [bk])
                        srg.append((b, bk))
                d4, d4k = next_chunk(27)
                d4v = d4.rearrange("p (f k j) -> p f k j", f=8, k=4)
                wax, waxk = next_chunk(16)
                wav = wax[:, 0:2048].rearrange("p (h k f) -> p h k f", h=4, k=2)
                wxv = wax[:, 2048:4096].rearrange("p (h k f) -> p h k f", h=4, k=2)
                xc32_h = {}
                xcb_h = {}

                def lfp(hd, k):
                    i = (hd % 2) * 8 + k
                    return FP[:, i, :], f"FP{i}"

                def lru_s1(hd):
                    xc32 = []
                    xcb = []
                    for f2 in range(2):
                        ft = hd * 2 + f2
                        ps, pk = pbank()
                        for k in range(4):
                            mm(ps, d4v[:, ft, k, :], R[:, ft, k:k + T], k == 0, k == 3, [f"R{ft}", d4k], [pk])
                        a, ak = lfp(hd, f2)
                        ts("dve", a, ps, vcol(32, ft), ALU.add, [pk, "VEC"], [ak])
                        b, bk = btile()
                        cp("dve", b, a, [ak], [bk])
                        xc32.append((a, ak))
                        xcb.append((b, bk))
                    xc32_h[hd] = xc32
                    xcb_h[hd] = xcb

                rg_h = {}
                ig_h = {}

                def lru_s2a(hd):
                    xcb = xcb_h[hd]
                    rg = []
                    ig = []
                    for f2 in range(2):
                        ft = hd * 2 + f2
                        psr, pkr = pbank()
                        for k2 in range(2):
                            mm(psr, wav[:, hd, k2, f2 * 128:(f2 + 1) * 128], xcb[k2][0], k2 == 0, k2 == 1,
                               [xcb[k2][1], waxk], [pkr])
                        psi, pki = pbank()
                        for k2 in range(2):
                            mm(psi, wxv[:, hd, k2, f2 * 128:(f2 + 1) * 128], xcb[k2][0], k2 == 0, k2 == 1,
                               [xcb[k2][1], waxk], [pki])
                        a, ak = lfp(hd, 2 + f2)
                        act(a, psr, AF.Tanh, [pkr, "DV"], [ak], bias=dcol(24, ft), scale=0.5)
                        b, bk = lfp(hd, 4 + f2)
                        act(b, psi, AF.Tanh, [pki, "DV"], [bk], bias=dcol(32, ft), scale=0.5)
                        rg.append((a, ak))
                        ig.append((b, bk))
                    rg_h[hd] = rg
                    ig_h[hd] = ig

                def lru_s2b(hd):
                    xc32 = xc32_h[hd]
                    rg = rg_h[hd]
                    ig = ig_h[hd]
                    avs = []
                    for f2 in range(2):
                        ft = hd * 2 + f2
                        av, avk = lfp(hd, 6 + f2)
                        act(av, rg[f2][0], AF.Exp, [rg[f2][1], "DV"], [avk], scale=dcol(16, ft), bias=dcol(16, ft))
                        a2, a2k = rg[f2]
                        tt("dve", a2, av, av, ALU.mult, [avk], [a2k])
                        avs.append((av, avk))
                    for f2 in range(2):
                        ft = hd * 2 + f2
                        av, avk = avs[f2]
                        a2, a2k = rg[f2]
                        act(a2, a2, AF.Sqrt, [a2k, "EPSD"], [a2k], bias=EPSD[:, 2:3], scale=-1.0)
                        bt, btk = ig[f2]
                        stt(bt, bt, 1.0, a2, ALU.add, ALU.mult, [btk, a2k], [btk])
                        stt(bt, bt, 0.5, xc32[f2][0], ALU.mult, ALU.mult, [btk, xc32[f2][1]], [btk])
                        hs, hsk = a2, a2k
                        P.add("dve", lambda e, av=av, bt=bt, hs=hs, ini=HST[:, l, ft:ft + 1]: e.tensor_tensor_scan(
                            hs, av, bt, ini, ALU.mult, ALU.add), [avk, btk, f"HST{l}_{ft}"], [hsk])
                        cp("pool", HST[:, l, ft:ft + 1], hs[:, T - 1:T], [hsk], [f"HST{l}_{ft}"])
                        tt("pool", UL[:, ft, :], hs, srg[ft][0], ALU.mult, [hsk, srg[ft][1]], [f"UL{ft}"])

                lru_s1(0)
                lru_s1(1)
                lru_s2a(0)
                conv31(0, keep=2)
                lru_s2b(0)
                lru_s1(2)
                lru_s2a(1)
                conv31(1, keep=3)
                lru_s2b(1)
                lru_s1(3)
                lru_s2a(2)
                lru_s2a(3)
                conv31(2, keep=1)
                conv31(3, keep=1)
                conv_ln_tail()
                if DBG:
                    dump("UC", UC[:], [f"UC{g}" for g in range(4)], BF16)
                    dump("C", C[:], [f"C{g}" for g in range(4)], BF16)

                if DBG:
                    dump("UL", UL[:], [f"UL{g}" for g in range(8)], BF16)
                    dump("DV", DV[:], ["DV"], F32)
                branches = [
                    (17, (9, 10), None, UP, 4, "UP"),
                    (18, (11, 12), None, UC, 4, "UC"),
                    ((19, 20), (13, 14), None, UL, 8, "UL"),
                ]
                for bi, (wy, wg, _, U, nk, uname) in enumerate(branches):
                    for half in range(2):
                        wgc, wgck = next_chunk(wg[half])
                        wgv = wgc.rearrange("p (k f) -> p k f", k=NKT)
                        if bi < 2 and half == 0:
                            wyc, wyck = next_chunk(wy)
                            wyv = wyc.rearrange("p (k f) -> p k f", k=nk)
                        if bi == 2:
                            wyc, wyck = next_chunk(wy[half])
                            wyv = wyc.rearrange("p (k f) -> p k f", k=nk)
                        for j4 in range(4):
                            j = half * 4 + j4
                            psg, pkg = pbank()
                            for kt in range(NKT):
                                mm(psg, wgv[:, kt, j4 * 128:(j4 + 1) * 128], H[:, kt, :], kt == 0, kt == NKT - 1,
                                   [Hk[kt], wgck], [pkg])
                            psy, pky = pbank()
                            ycol = j4 * 128 if bi == 2 else j * 128
                            for kt in range(nk):
                                mm(psy, wyv[:, kt, ycol:ycol + 128], U[:, kt, :], kt == 0, kt == nk - 1,
                                   [f"{uname}{kt}", wyck], [pky])
                            ci = ccnt[0] % 2
                            ccnt[0] += 1
                            f, fk = CF[:, ci, :], f"CF{ci}"
                            act(f, psg, AF.Sigmoid, [pkg], [fk])
                            if bi == 0:
                                tt("dve", M[:, j, :], psy, f, ALU.mult, [pky, fk], [f"M{j}"])
                            else:
                                tt("dve", f, psy, f, ALU.mult, [pky, fk], [fk])
                                if bi == 1:
                                    tt("pool", M[:, j, :], M[:, j, :], f, ALU.add, [f"M{j}", fk], [f"M{j}"])
                                else:
                                    tt("pool", MB[:, j, :], M[:, j, :], f, ALU.add, [f"M{j}", fk], [f"MB{j}"])
                            if bi == 0 and j == 1:
                                lru_s2b(2)
                            if bi == 0 and j == 5:
                                lru_s2b(3)

                if DBG:
                    dump("MB", MB[:], [f"MB{g}" for g in range(8)], BF16)
                sq = []
                for half in range(2):
                    wo, wok = next_chunk(21 + half)
                    wov = wo.rearrange("p (k f) -> p k f", k=NKT)
                    for j4 in range(4):
                        j = half * 4 + j4
                        ps, pk = pbank()
                        for kt in range(NKT):
                            mm(ps, wov[:, kt, j4 * 128:(j4 + 1) * 128], MB[:, kt, :], kt == 0, kt == NKT - 1,
                               [f"MB{kt}", wok], [pk])
                        act(M[:, j, :], ps, AF.Copy, [pk], [f"M{j}"])
                        act(SQR[:, j, :], ps, AF.Square, [pk], [f"SQR{j}"])
                        sq.append((SQR[:, j, :], f"SQR{j}"))
                ps, pk = sumsq_rstd(sq, None)
                act(RSTD[:], ps, AF.Ln, [pk, "EPSD"], ["RSTD"], bias=EPSD[:, 0:1])
                act(RSTD[:], RSTD[:], AF.Exp, ["RSTD"], ["RSTD"], scale=-0.5)
                for j in range(NKT):
                    stt(M[:, j, :], M[:, j, :], dcol(8, j), RSTD[:], ALU.mult, ALU.mult, [f"M{j}", "DV", "RSTD"], [f"M{j}"])
                    tt("dve", X[:, j, :], X[:, j, :], M[:, j, :], ALU.add, [f"X{j}", f"M{j}"], [f"X{j}"])
            dma(oT[:, :, tok0:tok0 + T].rearrange("k p t -> p k t"), X[:], "o", [f"X{k}" for k in range(NKT)], [])

        assert sstate["next"] == len(stream)
        flush_wb()
        P.emit(nc, sems, (["o", "dbg"] if debug else ["o"]) + [f"wb{i}" for i in range(RING) if f"wb{i}" in P.dma_cnt])
    return nc


def _band_consts():
    cst = np.zeros((128, NCONST), np.float32)
    cst[:, 0:128] = np.eye(128, dtype=np.float32)
    cst[:, 128:256] = 1.0
    tp = np.arange(128)[:, None]
    t = np.arange(128)[None, :]
    for g, w in enumerate((2, 4, 8, 16)):
        lag = t - tp
        b0 = np.where((lag >= 0) & (lag <= w - 1), 1.0 / w, 0.0) - (lag == 0)
        lag1 = t + 128 - tp
        b1 = np.where((lag1 >= 1) & (lag1 <= w - 1), 1.0 / w, 0.0)
        cnt = np.minimum(t + 1, w).astype(np.float64)
        b0f = np.where((lag >= 0) & (lag <= w - 1), 1.0 / cnt, 0.0) - (lag == 0)
        cst[:, 256 + g * 128:256 + (g + 1) * 128] = b0
        cst[:, 768 + g * 128:768 + (g + 1) * 128] = b1
        cst[:, 1280 + g * 128:1280 + (g + 1) * 128] = b0f
    return cst


def _pack(inputs, depth):
    f = lambda k: np.asarray(inputs[k], dtype=np.float32)
    w_in = f("w_in")
    wpk = np.zeros((depth, NPK, 128, CH), np.float32)
    vecs = np.zeros((128, DEPTH * NVL), np.float32)

    def kmaj(w):
        K, N = w.shape
        return w.reshape(K // 128, 128, N).transpose(1, 0, 2).reshape(128, -1)

    def colv(v):
        return v.reshape(-1, 128).T

    for l in range(depth):
        for c in range(15):
            wpk[l, c] = kmaj(w_in[l][:, c * 512:(c + 1) * 512])
        wpk[l, 15, :, 0:512] = f("pool_w")[l].transpose(1, 0, 2).reshape(128, 512)
        wa = f("lru_wa")[l].reshape(4, 2, 128, 256).transpose(2, 0, 1, 3).reshape(128, 2048)
        wx = f("lru_wx")[l].reshape(4, 2, 128, 256).transpose(2, 0, 1, 3).reshape(128, 2048)
        wpk[l, 16, :, 0:2048] = wa
        wpk[l, 16, :, 2048:4096] = wx
        wpk[l, 17] = kmaj(f("w_pool_out")[l])
        wpk[l, 18] = kmaj(f("w_conv_out")[l])
        wpk[l, 19] = kmaj(f("w_lru_out")[l][:, 0:512])
        wpk[l, 20] = kmaj(f("w_lru_out")[l][:, 512:1024])
        wpk[l, 21] = kmaj(f("w_out")[l][:, 0:512])
        wpk[l, 22] = kmaj(f("w_out")[l][:, 512:1024])
        vb = l * NVL
        vecs[:, vb + 0:vb + 8] = colv(f("norm_pre")[l])
        vecs[:, vb + 8:vb + 16] = colv(f("norm_post")[l])
        vecs[:, vb + 16:vb + 20] = colv(f("pool_scale")[l])
        vecs[:, vb + 20:vb + 24] = colv(f("conv_b")[l])
        vecs[:, vb + 24:vb + 28] = colv(f("conv_ln_g")[l])
        vecs[:, vb + 28:vb + 32] = colv(f("conv_ln_b")[l])
        vecs[:, vb + 32:vb + 40] = colv(f("lru_conv_b")[l])
        vecs[:, vb + 40:vb + 48] = colv(f("lru_ba")[l])
        vecs[:, vb + 48:vb + 56] = colv(f("lru_bx")[l])
        vecs[:, vb + 56:vb + 64] = colv(f("lru_lambda")[l])
        cw = f("conv_dw")[l]
        vecs[:, vb + 64:vb + 188] = cw.reshape(31, 4, 128).transpose(2, 1, 0).reshape(128, 124)
        lw = f("lru_conv_w")[l]
        vecs[:, vb + 188:vb + 220] = lw.reshape(4, 8, 128).transpose(2, 1, 0).reshape(128, 32)
    return wpk, vecs


def run(inputs, n_tiles=SEQ // T, depth=DEPTH, n_cores=8, trace=False, debug=False):
    x = np.asarray(inputs["x"], dtype=np.float32)
    ntok = n_tiles * T
    wpk, vecs = _pack(inputs, depth)
    cst = _band_consts()
    nc = build_nc(n_tiles, depth, debug)
    in_maps = []
    for b in range(n_cores):
        xT = np.ascontiguousarray(x[b, :ntok, :].T).reshape(NKT, 128, ntok)
        in_maps.append({"xT": xT, "wpk": wpk, "vecs": vecs, "cst": cst})
    res = run_bass_kernel_spmd(nc, in_maps, core_ids=list(range(n_cores)), trace=trace)
    out = np.stack([r["oT"].reshape(D, ntok).T for r in res.results], axis=0)
    return out, res


def kernel(**inputs):
    out, _ = run(inputs)
    return np.ascontiguousarray(out.astype(np.float32))
```

```python
import numpy as np
import concourse.bass as bass
import concourse.mybir as mybir
from concourse.bass_utils import run_bass_kernel_spmd

F32 = mybir.dt.float32
F32R = mybir.dt.float32r
BF16 = mybir.dt.bfloat16
AF = mybir.ActivationFunctionType
ALU = mybir.AluOpType

D = 1024
SEQ = 4096
DEPTH = 4
T = 512
NKT = 8
INW = 7680
EPS = 1e-6
NVL = 220
NCONST = 1792
CH = 4096
RING = 5
NF32 = 16
NSTG = 3
NDT = 8
NBF = 14
SAME_ENGINE_SYNC = True

NPK = 23
PK_USED = [CH] * 15 + [512, CH, CH, CH, CH, CH, CH, CH]
NSC = 28
SC_USED = PK_USED + [31 * 128] * 4 + [CH]


class Op:
    __slots__ = ("eng", "fn", "deps", "sig", "val", "dma_key", "uid")

    def __init__(self, eng, fn, dma_key=None, uid=0):
        self.eng = eng
        self.fn = fn
        self.deps = []
        self.sig = dma_key is not None
        self.val = 0
        self.dma_key = dma_key
        self.uid = uid


class Prog:
    ENGS = ("pe", "act", "dve", "pool", "sp")

    def __init__(self):
        self.ops = {e: [] for e in self.ENGS}
        self.bufs = {}
        self.n = 0
        self.dma_cnt = {}

    def _dep(self, op, prod, raw):
        if prod is op:
            return
        if prod.dma_key is None and prod.eng == op.eng:
            if op.eng == "pe" or not raw or not SAME_ENGINE_SYNC:
                return
        prod.sig = True
        if prod not in op.deps:
            op.deps.append(prod)

    def add(self, eng, fn, reads=(), writes=(), dma_key=None):
        self.n += 1
        op = Op(eng, fn, dma_key, self.n)
        for k in reads:
            b = self.bufs.setdefault(k, {"w": {}, "r": {}})
            for w in b["w"].values():
                self._dep(op, w, True)
        for k in writes:
            b = self.bufs.setdefault(k, {"w": {}, "r": {}})
            for w in b["w"].values():
                self._dep(op, w, False)
            for r in b["r"].values():
                self._dep(op, r, False)
        rk = eng if dma_key is None else ("dma", op.uid)
        for k in reads:
            self.bufs[k]["r"][rk] = op
        for k in writes:
            b = self.bufs[k]
            b["w"] = {rk: op}
            b["r"] = {}
        if dma_key is not None:
            self.dma_cnt[dma_key] = self.dma_cnt.get(dma_key, 0) + 16
            op.val = self.dma_cnt[dma_key]
        self.ops[eng].append(op)
        return op

    def emit(self, nc, sems, final_waits):
        for e in self.ENGS:
            c = 0
            for op in self.ops[e]:
                if op.dma_key is None and op.sig:
                    c += 1
                    op.val = c
        engs = {"pe": "tensor", "act": "scalar", "dve": "vector", "pool": "gpsimd", "sp": "sync"}

        def run(ename):
            def body(eng):
                waited = {}
                for op in self.ops[ename]:
                    for d in op.deps:
                        key = d.dma_key if d.dma_key is not None else d.eng
                        if waited.get(key, 0) >= d.val:
                            continue
                        eng.wait_ge(sems[key], d.val)
                        waited[key] = d.val
                    ins = op.fn(eng)
                    if op.dma_key is not None:
                        ins.then_inc(sems[op.dma_key], 16)
                    elif op.sig:
                        ins.then_inc(sems[ename], 1)
                if ename == "sp":
                    for key in final_waits:
                        eng.wait_ge(sems[key], self.dma_cnt[key])
            return body

        with nc.Block() as block:
            for ename in self.ENGS:
                if self.ops[ename]:
                    getattr(block, engs[ename])(run(ename))


def build_nc(n_tiles=SEQ // T, depth=DEPTH, debug=False):
    ntok = n_tiles * T
    nc = bass.Bass("TRN2", target_bir_lowering=False, dynamic_dma_scratch_size=1024)
    xT = nc.dram_tensor("xT", [NKT, 128, ntok], F32, kind="ExternalInput").ap()
    wpk = nc.dram_tensor("wpk", [depth, NPK, 128, CH], F32, kind="ExternalInput").ap()
    vecs_d = nc.dram_tensor("vecs", [128, DEPTH * NVL], F32, kind="ExternalInput").ap()
    cst_d = nc.dram_tensor("cst", [128, NCONST], F32, kind="ExternalInput").ap()
    oT = nc.dram_tensor("oT", [NKT, 128, ntok], F32, kind="ExternalOutput").ap()
    wsc = nc.dram_tensor("wsc", [depth, NSC, 128, CH], BF16, kind="Internal").ap()

    P = Prog()
    dbg = {}

    def dump(name, ap, keys, dt):
        if not debug:
            return
        shp = list(ap.shape)
        d_ = nc.dram_tensor("dbg_" + name, shp, dt, kind="ExternalOutput").ap()
        dbg[name] = d_
        P.add("sp", lambda e: e.dma_start(out=d_, in_=ap), keys, [], dma_key="dbg")
    import contextlib
    es = contextlib.ExitStack()
    with es:
        def sb(name, shape, dt):
            return es.enter_context(nc.sbuf_tensor(name, shape, dt))

        X = sb("X", [128, NKT, T], F32)
        H = sb("H", [128, NKT, T], BF16)
        UP = sb("UP", [128, 4, T], BF16)
        UC = sb("UC", [128, 4, T], BF16)
        UL = sb("UL", [128, 8, T], BF16)
        MB = sb("MB", [128, 8, T], BF16)
        M = sb("M", [128, 8, T], F32)
        RSTD = sb("RSTD", [128, T], F32)
        PTOKS = sb("PTOKS", [128, DEPTH, T], BF16)
        CHS = sb("CHS", [128, DEPTH, 4, 32], BF16)
        RHS = sb("RHS", [128, DEPTH, 8, 4], BF16)
        HST = sb("HST", [128, DEPTH, 8], F32)
        C = sb("C", [128, 4, T + 32], BF16)
        R = sb("R", [128, 8, T + 4], BF16)
        FP = sb("FP", [128, NF32, T], F32)
        BP = sb("BP", [128, NBF, T], BF16)
        RNG = sb("RNG", [128, RING, CH], BF16)
        VEC = sb("VEC", [128, DEPTH * NVL], F32)
        DV = sb("DV", [128, DEPTH * 40], F32)
        TMPV = sb("TMPV", [128, 4 * 32], F32)
        CST = sb("CST", [128, 128], F32)
        ONES = sb("ONES", [128, 128], F32R)
        EPSD = sb("EPSD", [128, 4], F32)
        SQR = sb("SQR", [128, 8, T], F32R)
        STG = sb("STG", [128, NSTG, CH // 2], F32)
        CF = sb("CF", [128, 2, T], F32)
        BAND = sb("BAND", [128, 12, 128], BF16)
        PS = es.enter_context(nc.psum_tensor("PS", [128, 8, T], F32))

        sem_names = ["pe", "act", "dve", "pool", "sp", "x", "o", "vec", "cst", "sf0", "sf1", "sf2", "dbg"] + [f"wb{i}" for i in range(RING)] + \
                    [f"ring{i}" for i in range(RING)]
        sems = {k: es.enter_context(nc.semaphore(k)) for k in sem_names}

        fcnt = [0]
        ccnt = [0]
        bcnt = [0]
        pcnt = [0]

        def ftile():
            i = fcnt[0] % NF32
            fcnt[0] += 1
            return FP[:, i, :], f"FP{i}"

        def btile():
            i = bcnt[0] % NBF
            bcnt[0] += 1
            return BP[:, i, :], f"BP{i}"

        def pbank():
            i = pcnt[0] % 8
            pcnt[0] += 1
            return PS[:, i, :], f"PS{i}"

        def mm(out, lhsT, rhs, start, stop, reads, writes):
            return P.add("pe", lambda e: e.matmul(out, lhsT, rhs, start=start, stop=stop), reads, writes)

        def act(out, in_, func, reads, writes, bias=None, scale=None):
            kw = {}
            if bias is not None:
                kw["bias"] = bias
            if scale is not None:
                kw["scale"] = scale
            return P.add("act", lambda e: e.activation(out, in_, func, **kw), reads, writes)

        def tt(eng, out, in0, in1, op, reads, writes):
            return P.add(eng, lambda e: e.tensor_tensor(out, in0, in1, op), reads, writes)

        def ts(eng, out, in0, s1, op0, reads, writes, s2=None, op1=None):
            if op1 is None:
                return P.add(eng, lambda e: e.tensor_scalar(out, in0, s1, None, op0), reads, writes)
            return P.add(eng, lambda e: e.tensor_scalar(out, in0, s1, s2, op0, op1), reads, writes)

        def stt(out, in0, scalar, in1, op0, op1, reads, writes):
            return P.add("dve", lambda e: e.scalar_tensor_tensor(out, in0, scalar, in1, op0, op1), reads, writes)

        def cp(eng, out, in_, reads, writes):
            return P.add(eng, lambda e: e.tensor_copy(out, in_), reads, writes)

        def dma(out, in_, key, reads, writes, eng="sp"):
            return P.add(eng, lambda e: e.dma_start(out=out, in_=in_), reads, writes, dma_key=key)

        dma(VEC[:], vecs_d[:, :], "vec", [], ["VEC"])
        dma(CST[:], cst_d[:, 0:128], "cst", [], ["CST"])
        dma(STG[:, 0, 0:1536], cst_d[:, 256:1792], "sf0", [], ["STG0"])
        dma(STG[:, 1, 0:128], cst_d[:, 128:256], "sf1", [], ["STG1"])
        cp("dve", ONES[:], STG[:, 1, 0:128], ["STG1"], ["ONES"])
        cp("dve", BAND[:].rearrange("p a b -> p (a b)"), STG[:, 0, 0:1536], ["STG0"], ["BAND"])
        IDENT = CST[:, 0:128]
        P.add("pool", lambda e: e.memset(EPSD[:, 0:1], float(D * EPS)), [], ["EPSD"])
        P.add("pool", lambda e: e.memset(EPSD[:, 1:2], float(EPS)), [], ["EPSD"])
        P.add("pool", lambda e: e.memset(EPSD[:, 2:3], 1.0), [], ["EPSD"])

        for l in range(depth):
            vb = l * NVL
            db = l * 40
            ts("dve", DV[:, db:db + 8], VEC[:, vb:vb + 8], 32.0, ALU.mult, ["VEC"], ["DV"])
            ts("dve", DV[:, db + 8:db + 16], VEC[:, vb + 8:vb + 16], 32.0, ALU.mult, ["VEC"], ["DV"])
            tb = l * 32
            e_ = TMPV[:, tb:tb + 8]
            u_ = TMPV[:, tb + 8:tb + 16]
            d_ = TMPV[:, tb + 16:tb + 24]
            q_ = TMPV[:, tb + 24:tb + 32]
            act(e_, VEC[:, vb + 56:vb + 64], AF.Exp, ["VEC"], ["TMPV"], scale=-1.0)
            ts("dve", u_, e_, 1.0, ALU.add, ["TMPV"], ["TMPV"])
            ts("dve", d_, u_, -1.0, ALU.add, ["TMPV"], ["TMPV"], s2=1e-30, op1=ALU.max)
            P.add("dve", lambda e, d_=d_: e.reciprocal(d_, d_), ["TMPV"], ["TMPV"])
            tt("dve", q_, e_, d_, ALU.mult, ["TMPV"], ["TMPV"])
            act(u_, u_, AF.Ln, ["TMPV"], ["TMPV"])
            tt("dve", u_, u_, q_, ALU.mult, ["TMPV"], ["TMPV"])
            ts("dve", DV[:, db + 16:db + 24], u_, -4.0, ALU.mult, ["TMPV"], ["DV"])
            ts("dve", DV[:, db + 24:db + 32], VEC[:, vb + 40:vb + 48], 0.5, ALU.mult, ["VEC"], ["DV"])
            ts("dve", DV[:, db + 32:db + 40], VEC[:, vb + 48:vb + 56], 0.5, ALU.mult, ["VEC"], ["DV"])

        HC = CH // 2

        P.add("pool", lambda e: e.memset(CHS[:].rearrange("p a b c -> p (a b c)"), 0.0), [], ["CHS"])
        P.add("pool", lambda e: e.memset(RHS[:].rearrange("p a b c -> p (a b c)"), 0.0), [], ["RHS"])
        P.add("pool", lambda e: e.memset(HST[:].rearrange("p a b -> p (a b)"), 0.0), [], [f"HST{l}_{ft}" for l in range(DEPTH) for ft in range(8)])

        stream = []
        order = [0, 1, 15, 2, 3, 4, 23, 24, 25, 26, 5, 6, 7, 8, 27, 16, 9, 17, 10, 11, 18, 12, 13, 19, 14, 20, 21, 22]
        for it in range(n_tiles):
            for l in range(depth):
                for c in order:
                    stream.append((l, c))
        sstate = {"issued": 0, "next": 0}

        pending_wb = []
        cast_rot = [0]
        n_first = depth * len(order)

        def flush_wb():
            for (l, c, s, used) in pending_wb:
                dma(wsc[l, c, :, 0:used], RNG[:, s, 0:used], f"wb{s}", [f"RNG{s}"], [f"WSC{l}_{c}"], eng="act")
            pending_wb.clear()

        staged = {}

        def stage_in(i):
            if i in staged or i >= n_first:
                return
            l, c = stream[i]
            lst = []
            if c < NPK:
                used = SC_USED[c]
                for hh in range(2):
                    lo = hh * HC
                    if lo >= used:
                        continue
                    w = min(HC, used - lo)
                    q = cast_rot[0] % NSTG
                    cast_rot[0] += 1
                    dma(STG[:, q, 0:w], wpk[l, c, :, lo:lo + w], f"sf{q}", [], [f"STG{q}"])
                    lst.append((q, lo, w))
            staged[i] = lst

        def fill_from_fp32(i, l, c, s):
            used = SC_USED[c]
            vb = l * NVL
            stage_in(i)
            if c < NPK:
                for (q, lo, w) in staged[i]:
                    act(RNG[:, s, lo:lo + w], STG[:, q, 0:w], AF.Copy, [f"STG{q}"], [f"RNG{s}"])
            elif c < 27:
                ct = c - 23
                for k in range(31):
                    col = vb + 64 + ct * 31 + k
                    ts("dve", RNG[:, s, k * 128:(k + 1) * 128], IDENT, VEC[:, col:col + 1],
                       ALU.mult, ["CST", "VEC"], [f"RNG{s}"])
            else:
                for j in range(32):
                    col = vb + 188 + j
                    ts("dve", RNG[:, s, j * 128:(j + 1) * 128], IDENT, VEC[:, col:col + 1],
                       ALU.mult, ["CST", "VEC"], [f"RNG{s}"])
            pending_wb.append((l, c, s, used))
            nxt = i + 1
            while nxt < n_first and stream[nxt][1] >= NPK:
                nxt += 1
            stage_in(nxt)

        def issue_loads(upto):
            while sstate["issued"] < min(upto, len(stream)):
                i = sstate["issued"]
                l, c = stream[i]
                s = i % RING
                used = SC_USED[c]
                flush_wb()
                if i < n_first:
                    fill_from_fp32(i, l, c, s)
                else:
                    dma(RNG[:, s, 0:used], wsc[l, c, :, 0:used], f"ring{s}", [f"WSC{l}_{c}"], [f"RNG{s}"])
                sstate["issued"] += 1

        def next_chunk(expect):
            i = sstate["next"]
            assert stream[i][1] == expect, (stream[i], expect)
            issue_loads(i + RING - 1)
            sstate["next"] += 1
            s = i % RING
            return RNG[:, s, :], f"RNG{s}"

        def sumsq_rstd(src_list, n_feat_scale_eps):
            ps, pk = pbank()
            for i, (a, k) in enumerate(src_list):
                mm(ps, ONES[:], a, i == 0, i == len(src_list) - 1, ["ONES", k], [pk])
            return ps, pk

        for it in range(n_tiles):
            tok0 = it * T
            dma(X[:], xT[:, :, tok0:tok0 + T].rearrange("k p t -> p k t"), "x", [], [f"X{k}" for k in range(NKT)])
            for l in range(depth):
                vb = l * NVL
                db = l * 40

                def vcol(off, i=0):
                    return VEC[:, vb + off + i:vb + off + i + 1]

                def dcol(off, i=0):
                    return DV[:, db + off + i:db + off + i + 1]

                sq = []
                for kt in range(NKT):
                    act(SQR[:, kt, :], X[:, kt, :], AF.Square, [f"X{kt}"], [f"SQR{kt}"])
                    sq.append((SQR[:, kt, :], f"SQR{kt}"))
                ps, pk = sumsq_rstd(sq, None)
                act(RSTD[:], ps, AF.Ln, [pk, "EPSD"], ["RSTD"], bias=EPSD[:, 0:1])
                act(RSTD[:], RSTD[:], AF.Exp, ["RSTD"], ["RSTD"], scale=-0.5)
                for kt in range(NKT):
                    stt(H[:, kt, :], X[:, kt, :], dcol(0, kt), RSTD[:], ALU.mult, ALU.mult,
                        [f"X{kt}", "DV", "RSTD"], [f"H{kt}"])
                Hk = [f"H{kt}" for kt in range(NKT)]
                DBG = debug and it == debug - 1 and l == 0
                if DBG:
                    dump("H", H[:], Hk, BF16)
                    dump("RSTD", RSTD[:], ["RSTD"], F32)

                w0, w0k = next_chunk(0)
                w0v = w0.rearrange("p (k f) -> p k f", k=NKT)
                ptok = []
                for t4 in range(4):
                    ps, pk = pbank()
                    for kt in range(NKT):
                        mm(ps, H[:, kt, t4 * 128:(t4 + 1) * 128], w0v[:, kt, :], kt == 0, kt == NKT - 1,
                           [Hk[kt], w0k], [pk])
                    b, bk = btile()
                    act(b, ps, AF.Copy, [pk], [bk])
                    ptok.append((b, bk))
                w1, w1k = next_chunk(1)
                w1v = w1.rearrange("p (k f) -> p k f", k=NKT)
                sg = []
                for ft in range(4):
                    ps, pk = pbank()
                    for kt in range(NKT):
                        mm(ps, w1v[:, kt, ft * 128:(ft + 1) * 128], H[:, kt, :], kt == 0, kt == NKT - 1,
                           [Hk[kt], w1k], [pk])
                    b, bk = btile()
                    act(b, ps, AF.Silu, [pk], [bk])
                    sg.append((b, bk))
                pw, pwk = next_chunk(15)
                pwv = pw[:, 0:512].rearrange("p (g f) -> p g f", g=4)
                for g in range(4):
                    ps, pk = pbank()
                    for t4 in range(4):
                        first = (it == 0 and t4 == 0)
                        bidx = (8 + g) if first else g
                        cur, curk = ptok[t4]
                        mm(ps[:, t4 * 128:(t4 + 1) * 128], cur[:, g * 128:(g + 1) * 128], BAND[:, bidx, :],
                           True, first, [curk, "BAND"], [pk])
                        if not first:
                            if t4 == 0:
                                prv, prvk = PTOKS[:, l, :], f"PTOKS{l}"
                            else:
                                prv, prvk = ptok[t4 - 1]
                            mm(ps[:, t4 * 128:(t4 + 1) * 128], prv[:, g * 128:(g + 1) * 128], BAND[:, 4 + g, :],
                               False, True, [prvk, "BAND"], [pk])
                    pl, plk = btile()
                    act(pl, ps, AF.Copy, [pk], [plk])
                    ps2, pk2 = pbank()
                    mm(ps2, pwv[:, g, :], pl, True, True, [plk, pwk], [pk2])
                    stt(UP[:, g, :], ps2, vcol(16, g), sg[g][0], ALU.mult, ALU.mult, [pk2, "VEC", sg[g][1]], [f"UP{g}"])
                cp("pool", PTOKS[:, l, :], ptok[3][0], [ptok[3][1]], [f"PTOKS{l}"])
                if DBG:
                    dump("UP", UP[:], [f"UP{g}" for g in range(4)], BF16)

                cp("pool", C[:, :, 0:30], CHS[:, l, :, 0:30], ["CHS"], [f"C{ft}" for ft in range(4)])
                w2, w2k = next_chunk(2)
                w3, w3k = next_chunk(3)
                w2v = w2.rearrange("p (k f) -> p k f", k=NKT)
                w3v = w3.rearrange("p (k f) -> p k f", k=NKT)
                for ft in range(4):
                    psv, pkv = pbank()
                    for kt in range(NKT):
                        mm(psv, w2v[:, kt, ft * 128:(ft + 1) * 128], H[:, kt, :], kt == 0, kt == NKT - 1,
                           [Hk[kt], w2k], [pkv])
                    psg, pkg = pbank()
                    for kt in range(NKT):
                        mm(psg, w3v[:, kt, ft * 128:(ft + 1) * 128], H[:, kt, :], kt == 0, kt == NKT - 1,
                           [Hk[kt], w3k], [pkg])
                    f, fk = ftile()
                    act(f, psg, AF.Sigmoid, [pkg], [fk])
                    if DBG and ft == 3:
                        dump("SIG3", f, [fk], F32)
                        f2_, f2k_ = ftile()
                        cp("dve", f2_, psv, [pkv], [f2k_])
                        dump("CV3", f2_, [f2k_], F32)
                        f3_, f3k_ = ftile()
                        tt("dve", f3_, psv, f, ALU.mult, [pkv, fk], [f3k_])
                        dump("CM3", f3_, [f3k_], F32)
                    tt("dve", C[:, ft, 30:30 + T], psv, f, ALU.mult, [pkv, fk], [f"C{ft}"])
                cp("pool", CHS[:, l, :, 0:30], C[:, :, T:T + 30], [f"C{ft}" for ft in range(4)], ["CHS"])
                w4, w4k = next_chunk(4)
                w4v = w4.rearrange("p (k f) -> p k f", k=NKT)
                scg = []
                for ft in range(4):
                    ps, pk = pbank()
                    for kt in range(NKT):
                        mm(ps, w4v[:, kt, ft * 128:(ft + 1) * 128], H[:, kt, :], kt == 0, kt == NKT - 1,
                           [Hk[kt], w4k], [pk])
                    b, bk = btile()
                    act(b, ps, AF.Silu, [pk], [bk])
                    scg.append((b, bk))
                cc = []
                csq = []
                for ft in range(4):
                    acc, acck = M[:, ft, :], f"M{ft}"
                    for k in range(NDT):
                        wcol = vcol(64 + ft * 31 + k)
                        if k == 0:
                            ts("dve", acc, C[:, ft, k:k + T], wcol, ALU.mult, [f"C{ft}", "VEC"], [acck])
                        else:
                            stt(acc, C[:, ft, k:k + T], wcol, acc, ALU.mult, ALU.add, [f"C{ft}", "VEC", acck], [acck])
                for ft in range(4):
                    dd, ddk = next_chunk(23 + ft)
                    ddv = dd[:, 0:31 * 128].rearrange("p (k f) -> p k f", k=31)
                    ps, pk = pbank()
                    for k in range(NDT, 31):
                        mm(ps, ddv[:, k, :], C[:, ft, k:k + T], k == NDT, k == 30, [f"C{ft}", ddk], [pk])
                    acc, acck = M[:, ft, :], f"M{ft}"
                    a, ak = SQR[:, ft, :], f"SQR{ft}"
                    stt(a, ps, vcol(20, ft), acc, ALU.add, ALU.add, [pk, "VEC", acck], [ak])
                    a2, a2k = SQR[:, 4 + ft, :], f"SQR{4 + ft}"
                    act(a2, a.bitcast(F32), AF.Square, [ak], [a2k])
                    cc.append((a, ak))
                    csq.append((a2, a2k))
                psm, pkm = sumsq_rstd(cc, None)
                pss, pks = sumsq_rstd(csq, None)
                mean, meank = ftile()
                ts("dve", mean, psm, 1.0 / 512, ALU.mult, [pkm], [meank])
                m2, m2k = ftile()
                tt("dve", m2, mean, mean, ALU.mult, [meank], [m2k])
                var, vark = ftile()
                stt(var, pss, 1.0 / 512, m2, ALU.mult, ALU.subtract, [pks, m2k], [vark])
                act(var, var, AF.Ln, [vark, "EPSD"], [vark], bias=EPSD[:, 1:2])
                act(var, var, AF.Exp, [vark], [vark], scale=-0.5)
                for ft in range(4):
                    a0, a0k = cc[ft]
                    a, ak = ftile()
                    tt("dve", a, a0.bitcast(F32), mean, ALU.subtract, [a0k, meank], [ak])
                    tt("dve", a, a, var, ALU.mult, [ak, vark], [ak])
                    b, bk = btile()
                    act(b, a, AF.Silu, [ak, "VEC"], [bk], bias=vcol(28, ft), scale=vcol(24, ft))
                    tt("pool", UC[:, ft, :], b, scg[ft][0], ALU.mult, [bk, scg[ft][1]], [f"UC{ft}"])

                if DBG:
                    dump("UC", UC[:], [f"UC{g}" for g in range(4)], BF16)
                    dump("C", C[:], [f"C{g}" for g in range(4)], BF16)
                cp("pool", R[:, :, 0:3], RHS[:, l, :, 0:3], ["RHS"], [f"R{ft}" for ft in range(8)])
                for half in range(2):
                    w5, w5k = next_chunk(5 + half)
                    w5v = w5.rearrange("p (k f) -> p k f", k=NKT)
                    for f4 in range(4):
                        ft = half * 4 + f4
                        ps, pk = pbank()
                        for kt in range(NKT):
                            mm(ps, w5v[:, kt, f4 * 128:(f4 + 1) * 128], H[:, kt, :], kt == 0, kt == NKT - 1,
                               [Hk[kt], w5k], [pk])
                        act(R[:, ft, 3:3 + T], ps, AF.Copy, [pk], [f"R{ft}"])
                cp("pool", RHS[:, l, :, 0:3], R[:, :, T:T + 3], [f"R{ft}" for ft in range(8)], ["RHS"])
                srg = []
                for half in range(2):
                    w7, w7k = next_chunk(7 + half)
                    w7v = w7.rearrange("p (k f) -> p k f", k=NKT)
                    for f4 in range(4):
                        ps, pk = pbank()
                        for kt in range(NKT):
                            mm(ps, w7v[:, kt, f4 * 128:(f4 + 1) * 128], H[:, kt, :], kt == 0, kt == NKT - 1,
                               [Hk[kt], w7k], [pk])
                        b, bk = btile()
                        act(b, ps, AF.Silu, [pk], [bk])
                        srg.append((b, bk))
                d4, d4k = next_chunk(27)
                d4v = d4.rearrange("p (f k j) -> p f k j", f=8, k=4)
                wax, waxk = next_chunk(16)
                wav = wax[:, 0:2048].rearrange("p (h k f) -> p h k f", h=4, k=2)
                wxv = wax[:, 2048:4096].rearrange("p (h k f) -> p h k f", h=4, k=2)
                xc32_h = {}
                xcb_h = {}

                def lfp(hd, k):
                    i = (hd % 2) * 8 + k
                    return FP[:, i, :], f"FP{i}"

                def lru_s1(hd):
                    xc32 = []
                    xcb = []
                    for f2 in range(2):
                        ft = hd * 2 + f2
                        ps, pk = pbank()
                        for k in range(4):
                            mm(ps, d4v[:, ft, k, :], R[:, ft, k:k + T], k == 0, k == 3, [f"R{ft}", d4k], [pk])
                        a, ak = lfp(hd, f2)
                        ts("dve", a, ps, vcol(32, ft), ALU.add, [pk, "VEC"], [ak])
                        b, bk = btile()
                        cp("dve", b, a, [ak], [bk])
                        xc32.append((a, ak))
                        xcb.append((b, bk))
                    xc32_h[hd] = xc32
                    xcb_h[hd] = xcb

                rg_h = {}
                ig_h = {}

                def lru_s2a(hd):
                    xcb = xcb_h[hd]
                    rg = []
                    ig = []
                    for f2 in range(2):
                        ft = hd * 2 + f2
                        psr, pkr = pbank()
                        for k2 in range(2):
                            mm(psr, wav[:, hd, k2, f2 * 128:(f2 + 1) * 128], xcb[k2][0], k2 == 0, k2 == 1,
                               [xcb[k2][1], waxk], [pkr])
                        psi, pki = pbank()
                        for k2 in range(2):
                            mm(psi, wxv[:, hd, k2, f2 * 128:(f2 + 1) * 128], xcb[k2][0], k2 == 0, k2 == 1,
                               [xcb[k2][1], waxk], [pki])
                        a, ak = lfp(hd, 2 + f2)
                        act(a, psr, AF.Tanh, [pkr, "DV"], [ak], bias=dcol(24, ft), scale=0.5)
                        b, bk = lfp(hd, 4 + f2)
                        act(b, psi, AF.Tanh, [pki, "DV"], [bk], bias=dcol(32, ft), scale=0.5)
                        rg.append((a, ak))
                        ig.append((b, bk))
                    rg_h[hd] = rg
                    ig_h[hd] = ig

                def lru_s2b(hd):
                    xc32 = xc32_h[hd]
                    rg = rg_h[hd]
                    ig = ig_h[hd]
                    avs = []
                    for f2 in range(2):
                        ft = hd * 2 + f2
                        av, avk = lfp(hd, 6 + f2)
                        act(av, rg[f2][0], AF.Exp, [rg[f2][1], "DV"], [avk], scale=dcol(16, ft), bias=dcol(16, ft))
                        a2, a2k = rg[f2]
                        tt("dve", a2, av, av, ALU.mult, [avk], [a2k])
                        avs.append((av, avk))
                    for f2 in range(2):
                        ft = hd * 2 + f2
                        av, avk = avs[f2]
                        a2, a2k = rg[f2]
                        act(a2, a2, AF.Sqrt, [a2k, "EPSD"], [a2k], bias=EPSD[:, 2:3], scale=-1.0)
                        bt, btk = ig[f2]
                        stt(bt, bt, 1.0, a2, ALU.add, ALU.mult, [btk, a2k], [btk])
                        stt(bt, bt, 0.5, xc32[f2][0], ALU.mult, ALU.mult, [btk, xc32[f2][1]], [btk])
                        hs, hsk = a2, a2k
                        P.add("dve", lambda e, av=av, bt=bt, hs=hs, ini=HST[:, l, ft:ft + 1]: e.tensor_tensor_scan(
                            hs, av, bt, ini, ALU.mult, ALU.add), [avk, btk, f"HST{l}_{ft}"], [hsk])
                        cp("pool", HST[:, l, ft:ft + 1], hs[:, T - 1:T], [hsk], [f"HST{l}_{ft}"])
                        tt("pool", UL[:, ft, :], hs, srg[ft][0], ALU.mult, [hsk, srg[ft][1]], [f"UL{ft}"])

                lru_s1(0)
                lru_s1(1)
                lru_s2a(0)
                lru_s2b(0)
                lru_s1(2)
                lru_s2a(1)
                lru_s2b(1)
                lru_s1(3)
                lru_s2a(2)
                lru_s2a(3)

                if DBG:
                    dump("UL", UL[:], [f"UL{g}" for g in range(8)], BF16)
                    dump("DV", DV[:], ["DV"], F32)
                branches = [
                    (17, (9, 10), None, UP, 4, "UP"),
                    (18, (11, 12), None, UC, 4, "UC"),
                    ((19, 20), (13, 14), None, UL, 8, "UL"),
                ]
                for bi, (wy, wg, _, U, nk, uname) in enumerate(branches):
                    for half in range(2):
                        wgc, wgck = next_chunk(wg[half])
                        wgv = wgc.rearrange("p (k f) -> p k f", k=NKT)
                        if bi < 2 and half == 0:
                            wyc, wyck = next_chunk(wy)
                            wyv = wyc.rearrange("p (k f) -> p k f", k=nk)
                        if bi == 2:
                            wyc, wyck = next_chunk(wy[half])
                            wyv = wyc.rearrange("p (k f) -> p k f", k=nk)
                        for j4 in range(4):
                            j = half * 4 + j4
                            psg, pkg = pbank()
                            for kt in range(NKT):
                                mm(psg, wgv[:, kt, j4 * 128:(j4 + 1) * 128], H[:, kt, :], kt == 0, kt == NKT - 1,
                                   [Hk[kt], wgck], [pkg])
                            psy, pky = pbank()
                            ycol = j4 * 128 if bi == 2 else j * 128
                            for kt in range(nk):
                                mm(psy, wyv[:, kt, ycol:ycol + 128], U[:, kt, :], kt == 0, kt == nk - 1,
                                   [f"{uname}{kt}", wyck], [pky])
                            ci = ccnt[0] % 2
                            ccnt[0] += 1
                            f, fk = CF[:, ci, :], f"CF{ci}"
                            act(f, psg, AF.Sigmoid, [pkg], [fk])
                            if bi == 0:
                                tt("dve", M[:, j, :], psy, f, ALU.mult, [pky, fk], [f"M{j}"])
                            else:
                                tt("dve", f, psy, f, ALU.mult, [pky, fk], [fk])
                                if bi == 1:
                                    tt("pool", M[:, j, :], M[:, j, :], f, ALU.add, [f"M{j}", fk], [f"M{j}"])
                                else:
                                    tt("pool", MB[:, j, :], M[:, j, :], f, ALU.add, [f"M{j}", fk], [f"MB{j}"])
                            if bi == 0 and j == 1:
                                lru_s2b(2)
                            if bi == 0 and j == 5:
                                lru_s2b(3)

                if DBG:
                    dump("MB", MB[:], [f"MB{g}" for g in range(8)], BF16)
                sq = []
                for half in range(2):
                    wo, wok = next_chunk(21 + half)
                    wov = wo.rearrange("p (k f) -> p k f", k=NKT)
                    for j4 in range(4):
                        j = half * 4 + j4
                        ps, pk = pbank()
                        for kt in range(NKT):
                            mm(ps, wov[:, kt, j4 * 128:(j4 + 1) * 128], MB[:, kt, :], kt == 0, kt == NKT - 1,
                               [f"MB{kt}", wok], [pk])
                        act(M[:, j, :], ps, AF.Copy, [pk], [f"M{j}"])
                        act(SQR[:, j, :], ps, AF.Square, [pk], [f"SQR{j}"])
                        sq.append((SQR[:, j, :], f"SQR{j}"))
                ps, pk = sumsq_rstd(sq, None)
                act(RSTD[:], ps, AF.Ln, [pk, "EPSD"], ["RSTD"], bias=EPSD[:, 0:1])
                act(RSTD[:], RSTD[:], AF.Exp, ["RSTD"], ["RSTD"], scale=-0.5)
                for j in range(NKT):
                    stt(M[:, j, :], M[:, j, :], dcol(8, j), RSTD[:], ALU.mult, ALU.mult, [f"M{j}", "DV", "RSTD"], [f"M{j}"])
                    tt("dve" if j % 2 == 0 else "pool", X[:, j, :], X[:, j, :], M[:, j, :], ALU.add,
                       [f"X{j}", f"M{j}"], [f"X{j}"])
            dma(oT[:, :, tok0:tok0 + T].rearrange("k p t -> p k t"), X[:], "o", [f"X{k}" for k in range(NKT)], [])

        assert sstate["next"] == len(stream)
        flush_wb()
        P.emit(nc, sems, (["o", "dbg"] if debug else ["o"]) + [f"wb{i}" for i in range(RING) if f"wb{i}" in P.dma_cnt])
    return nc


def _band_consts():
    cst = np.zeros((128, NCONST), np.float32)
    cst[:, 0:128] = np.eye(128, dtype=np.float32)
    cst[:, 128:256] = 1.0
    tp = np.arange(128)[:, None]
    t = np.arange(128)[None, :]
    for g, w in enumerate((2, 4, 8, 16)):
        lag = t - tp
        b0 = np.where((lag >= 0) & (lag <= w - 1), 1.0 / w, 0.0) - (lag == 0)
        lag1 = t + 128 - tp
        b1 = np.where((lag1 >= 1) & (lag1 <= w - 1), 1.0 / w, 0.0)
        cnt = np.minimum(t + 1, w).astype(np.float64)
        b0f = np.where((lag >= 0) & (lag <= w - 1), 1.0 / cnt, 0.0) - (lag == 0)
        cst[:, 256 + g * 128:256 + (g + 1) * 128] = b0
        cst[:, 768 + g * 128:768 + (g + 1) * 128] = b1
        cst[:, 1280 + g * 128:1280 + (g + 1) * 128] = b0f
    return cst


def _pack(inputs, depth):
    f = lambda k: np.asarray(inputs[k], dtype=np.float32)
    w_in = f("w_in")
    wpk = np.zeros((depth, NPK, 128, CH), np.float32)
    vecs = np.zeros((128, DEPTH * NVL), np.float32)

    def kmaj(w):
        K, N = w.shape
        return w.reshape(K // 128, 128, N).transpose(1, 0, 2).reshape(128, -1)

    def colv(v):
        return v.reshape(-1, 128).T

    for l in range(depth):
        for c in range(15):
            wpk[l, c] = kmaj(w_in[l][:, c * 512:(c + 1) * 512])
        wpk[l, 15, :, 0:512] = f("pool_w")[l].transpose(1, 0, 2).reshape(128, 512)
        wa = f("lru_wa")[l].reshape(4, 2, 128, 256).transpose(2, 0, 1, 3).reshape(128, 2048)
        wx = f("lru_wx")[l].reshape(4, 2, 128, 256).transpose(2, 0, 1, 3).reshape(128, 2048)
        wpk[l, 16, :, 0:2048] = wa
        wpk[l, 16, :, 2048:4096] = wx
        wpk[l, 17] = kmaj(f("w_pool_out")[l])
        wpk[l, 18] = kmaj(f("w_conv_out")[l])
        wpk[l, 19] = kmaj(f("w_lru_out")[l][:, 0:512])
        wpk[l, 20] = kmaj(f("w_lru_out")[l][:, 512:1024])
        wpk[l, 21] = kmaj(f("w_out")[l][:, 0:512])
        wpk[l, 22] = kmaj(f("w_out")[l][:, 512:1024])
        vb = l * NVL
        vecs[:, vb + 0:vb + 8] = colv(f("norm_pre")[l])
        vecs[:, vb + 8:vb + 16] = colv(f("norm_post")[l])
        vecs[:, vb + 16:vb + 20] = colv(f("pool_scale")[l])
        vecs[:, vb + 20:vb + 24] = colv(f("conv_b")[l])
        vecs[:, vb + 24:vb + 28] = colv(f("conv_ln_g")[l])
        vecs[:, vb + 28:vb + 32] = colv(f("conv_ln_b")[l])
        vecs[:, vb + 32:vb + 40] = colv(f("lru_conv_b")[l])
        vecs[:, vb + 40:vb + 48] = colv(f("lru_ba")[l])
        vecs[:, vb + 48:vb + 56] = colv(f("lru_bx")[l])
        vecs[:, vb + 56:vb + 64] = colv(f("lru_lambda")[l])
        cw = f("conv_dw")[l]
        vecs[:, vb + 64:vb + 188] = cw.reshape(31, 4, 128).transpose(2, 1, 0).reshape(128, 124)
        lw = f("lru_conv_w")[l]
        vecs[:, vb + 188:vb + 220] = lw.reshape(4, 8, 128).transpose(2, 1, 0).reshape(128, 32)
    return wpk, vecs


def run(inputs, n_tiles=SEQ // T, depth=DEPTH, n_cores=8, trace=False, debug=False):
    x = np.asarray(inputs["x"], dtype=np.float32)
    ntok = n_tiles * T
    wpk, vecs = _pack(inputs, depth)
    cst = _band_consts()
    nc = build_nc(n_tiles, depth, debug)
    in_maps = []
    for b in range(n_cores):
        xT = np.ascontiguousarray(x[b, :ntok, :].T).reshape(NKT, 128, ntok)
        in_maps.append({"xT": xT, "wpk": wpk, "vecs": vecs, "cst": cst})
    res = run_bass_kernel_spmd(nc, in_maps, core_ids=list(range(n_cores)), trace=trace)
    out = np.stack([r["oT"].reshape(D, ntok).T for r in res.results], axis=0)
    return out, res


def kernel(**inputs):
    out, _ = run(inputs)
    return np.ascontiguousarray(out.astype(np.float32))
```

```python
import numpy as np
import concourse.bass as bass
import concourse.mybir as mybir
from concourse.bass_utils import run_bass_kernel_spmd

F32 = mybir.dt.float32
F32R = mybir.dt.float32r
BF16 = mybir.dt.bfloat16
AF = mybir.ActivationFunctionType
ALU = mybir.AluOpType

D = 1024
SEQ = 4096
DEPTH = 4
T = 512
NKT = 8
INW = 7680
EPS = 1e-6
NVL = 220
NCONST = 1792
CH = 4096
RING = 5
NF32 = 16
NSTG = 3
NDT = 4
NBF = 14
SAME_ENGINE_SYNC = True

NPK = 23
PK_USED = [CH] * 15 + [512, CH, CH, CH, CH, CH, CH, CH]
NSC = 28
SC_USED = PK_USED + [31 * 128] * 4 + [CH]


class Op:
    __slots__ = ("eng", "fn", "deps", "sig", "val", "dma_key", "uid")

    def __init__(self, eng, fn, dma_key=None, uid=0):
        self.eng = eng
        self.fn = fn
        self.deps = []
        self.sig = dma_key is not None
        self.val = 0
        self.dma_key = dma_key
        self.uid = uid


class Prog:
    ENGS = ("pe", "act", "dve", "pool", "sp")

    def __init__(self):
        self.ops = {e: [] for e in self.ENGS}
        self.bufs = {}
        self.n = 0
        self.dma_cnt = {}

    def _dep(self, op, prod, raw):
        if prod is op:
            return
        if prod.dma_key is None and prod.eng == op.eng:
            if op.eng == "pe" or not raw or not SAME_ENGINE_SYNC:
                return
        prod.sig = True
        if prod not in op.deps:
            op.deps.append(prod)

    def add(self, eng, fn, reads=(), writes=(), dma_key=None):
        self.n += 1
        op = Op(eng, fn, dma_key, self.n)
        for k in reads:
            b = self.bufs.setdefault(k, {"w": {}, "r": {}})
            for w in b["w"].values():
                self._dep(op, w, True)
        for k in writes:
            b = self.bufs.setdefault(k, {"w": {}, "r": {}})
            for w in b["w"].values():
                self._dep(op, w, False)
            for r in b["r"].values():
                self._dep(op, r, False)
        rk = eng if dma_key is None else ("dma", op.uid)
        for k in reads:
            self.bufs[k]["r"][rk] = op
        for k in writes:
            b = self.bufs[k]
            b["w"] = {rk: op}
            b["r"] = {}
        if dma_key is not None:
            self.dma_cnt[dma_key] = self.dma_cnt.get(dma_key, 0) + 16
            op.val = self.dma_cnt[dma_key]
        self.ops[eng].append(op)
        return op

    def emit(self, nc, sems, final_waits):
        for e in self.ENGS:
            c = 0
            for op in self.ops[e]:
                if op.dma_key is None and op.sig:
                    c += 1
                    op.val = c
        engs = {"pe": "tensor", "act": "scalar", "dve": "vector", "pool": "gpsimd", "sp": "sync"}

        def run(ename):
            def body(eng):
                waited = {}
                for op in self.ops[ename]:
                    for d in op.deps:
                        key = d.dma_key if d.dma_key is not None else d.eng
                        if waited.get(key, 0) >= d.val:
                            continue
                        eng.wait_ge(sems[key], d.val)
                        waited[key] = d.val
                    ins = op.fn(eng)
                    if op.dma_key is not None:
                        ins.then_inc(sems[op.dma_key], 16)
                    elif op.sig:
                        ins.then_inc(sems[ename], 1)
                if ename == "sp":
                    for key in final_waits:
                        eng.wait_ge(sems[key], self.dma_cnt[key])
            return body

        with nc.Block() as block:
            for ename in self.ENGS:
                if self.ops[ename]:
                    getattr(block, engs[ename])(run(ename))


def build_nc(n_tiles=SEQ // T, depth=DEPTH, debug=False):
    ntok = n_tiles * T
    nc = bass.Bass("TRN2", target_bir_lowering=False, dynamic_dma_scratch_size=1024)
    xT = nc.dram_tensor("xT", [NKT, 128, ntok], F32, kind="ExternalInput").ap()
    wpk = nc.dram_tensor("wpk", [depth, NPK, 128, CH], F32, kind="ExternalInput").ap()
    vecs_d = nc.dram_tensor("vecs", [128, DEPTH * NVL], F32, kind="ExternalInput").ap()
    cst_d = nc.dram_tensor("cst", [128, NCONST], F32, kind="ExternalInput").ap()
    oT = nc.dram_tensor("oT", [NKT, 128, ntok], F32, kind="ExternalOutput").ap()
    wsc = nc.dram_tensor("wsc", [depth, NSC, 128, CH], BF16, kind="Internal").ap()

    P = Prog()
    dbg = {}

    def dump(name, ap, keys, dt):
        if not debug:
            return
        shp = list(ap.shape)
        d_ = nc.dram_tensor("dbg_" + name, shp, dt, kind="ExternalOutput").ap()
        dbg[name] = d_
        P.add("sp", lambda e: e.dma_start(out=d_, in_=ap), keys, [], dma_key="dbg")
    import contextlib
    es = contextlib.ExitStack()
    with es:
        def sb(name, shape, dt):
            return es.enter_context(nc.sbuf_tensor(name, shape, dt))

        X = sb("X", [128, NKT, T], F32)
        H = sb("H", [128, NKT, T], BF16)
        UP = sb("UP", [128, 4, T], BF16)
        UC = sb("UC", [128, 4, T], BF16)
        UL = sb("UL", [128, 8, T], BF16)
        MB = sb("MB", [128, 8, T], BF16)
        M = sb("M", [128, 8, T], F32)
        RSTD = sb("RSTD", [128, T], F32)
        PTOKS = sb("PTOKS", [128, DEPTH, T], BF16)
        CHS = sb("CHS", [128, DEPTH, 4, 32], BF16)
        RHS = sb("RHS", [128, DEPTH, 8, 4], BF16)
        HST = sb("HST", [128, DEPTH, 8], F32)
        C = sb("C", [128, 4, T + 32], BF16)
        R = sb("R", [128, 8, T + 4], BF16)
        FP = sb("FP", [128, NF32, T], F32)
        BP = sb("BP", [128, NBF, T], BF16)
        RNG = sb("RNG", [128, RING, CH], BF16)
        VEC = sb("VEC", [128, DEPTH * NVL], F32)
        DV = sb("DV", [128, DEPTH * 40], F32)
        TMPV = sb("TMPV", [128, 4 * 32], F32)
        CST = sb("CST", [128, 128], F32)
        ONES = sb("ONES", [128, 128], F32R)
        EPSD = sb("EPSD", [128, 4], F32)
        SQR = sb("SQR", [128, 8, T], F32R)
        STG = sb("STG", [128, NSTG, CH // 2], F32)
        CF = sb("CF", [128, 2, T], F32)
        BAND = sb("BAND", [128, 12, 128], BF16)
        PS = es.enter_context(nc.psum_tensor("PS", [128, 8, T], F32))

        sem_names = ["pe", "act", "dve", "pool", "sp", "x", "o", "vec", "cst", "sf0", "sf1", "sf2", "dbg"] + [f"wb{i}" for i in range(RING)] + \
                    [f"ring{i}" for i in range(RING)]
        sems = {k: es.enter_context(nc.semaphore(k)) for k in sem_names}

        fcnt = [0]
        ccnt = [0]
        bcnt = [0]
        pcnt = [0]

        def ftile():
            i = fcnt[0] % NF32
            fcnt[0] += 1
            return FP[:, i, :], f"FP{i}"

        def btile():
            i = bcnt[0] % NBF
            bcnt[0] += 1
            return BP[:, i, :], f"BP{i}"

        def pbank():
            i = pcnt[0] % 8
            pcnt[0] += 1
            return PS[:, i, :], f"PS{i}"

        def mm(out, lhsT, rhs, start, stop, reads, writes):
            return P.add("pe", lambda e: e.matmul(out, lhsT, rhs, start=start, stop=stop), reads, writes)

        def act(out, in_, func, reads, writes, bias=None, scale=None):
            kw = {}
            if bias is not None:
                kw["bias"] = bias
            if scale is not None:
                kw["scale"] = scale
            return P.add("act", lambda e: e.activation(out, in_, func, **kw), reads, writes)

        def tt(eng, out, in0, in1, op, reads, writes):
            return P.add(eng, lambda e: e.tensor_tensor(out, in0, in1, op), reads, writes)

        def ts(eng, out, in0, s1, op0, reads, writes, s2=None, op1=None):
            if op1 is None:
                return P.add(eng, lambda e: e.tensor_scalar(out, in0, s1, None, op0), reads, writes)
            return P.add(eng, lambda e: e.tensor_scalar(out, in0, s1, s2, op0, op1), reads, writes)

        def stt(out, in0, scalar, in1, op0, op1, reads, writes):
            return P.add("dve", lambda e: e.scalar_tensor_tensor(out, in0, scalar, in1, op0, op1), reads, writes)

        def cp(eng, out, in_, reads, writes):
            return P.add(eng, lambda e: e.tensor_copy(out, in_), reads, writes)

        def dma(out, in_, key, reads, writes, eng="sp"):
            return P.add(eng, lambda e: e.dma_start(out=out, in_=in_), reads, writes, dma_key=key)

        dma(VEC[:], vecs_d[:, :], "vec", [], ["VEC"])
        dma(CST[:], cst_d[:, 0:128], "cst", [], ["CST"])
        dma(STG[:, 0, 0:1536], cst_d[:, 256:1792], "sf0", [], ["STG0"])
        dma(STG[:, 1, 0:128], cst_d[:, 128:256], "sf1", [], ["STG1"])
        cp("dve", ONES[:], STG[:, 1, 0:128], ["STG1"], ["ONES"])
        cp("dve", BAND[:].rearrange("p a b -> p (a b)"), STG[:, 0, 0:1536], ["STG0"], ["BAND"])
        IDENT = CST[:, 0:128]
        P.add("pool", lambda e: e.memset(EPSD[:, 0:1], float(D * EPS)), [], ["EPSD"])
        P.add("pool", lambda e: e.memset(EPSD[:, 1:2], float(EPS)), [], ["EPSD"])
        P.add("pool", lambda e: e.memset(EPSD[:, 2:3], 1.0), [], ["EPSD"])

        for l in range(depth):
            vb = l * NVL
            db = l * 40
            ts("dve", DV[:, db:db + 8], VEC[:, vb:vb + 8], 32.0, ALU.mult, ["VEC"], ["DV"])
            ts("dve", DV[:, db + 8:db + 16], VEC[:, vb + 8:vb + 16], 32.0, ALU.mult, ["VEC"], ["DV"])
            tb = l * 32
            e_ = TMPV[:, tb:tb + 8]
            u_ = TMPV[:, tb + 8:tb + 16]
            d_ = TMPV[:, tb + 16:tb + 24]
            q_ = TMPV[:, tb + 24:tb + 32]
            act(e_, VEC[:, vb + 56:vb + 64], AF.Exp, ["VEC"], ["TMPV"], scale=-1.0)
            ts("dve", u_, e_, 1.0, ALU.add, ["TMPV"], ["TMPV"])
            ts("dve", d_, u_, -1.0, ALU.add, ["TMPV"], ["TMPV"], s2=1e-30, op1=ALU.max)
            P.add("dve", lambda e, d_=d_: e.reciprocal(d_, d_), ["TMPV"], ["TMPV"])
            tt("dve", q_, e_, d_, ALU.mult, ["TMPV"], ["TMPV"])
            act(u_, u_, AF.Ln, ["TMPV"], ["TMPV"])
            tt("dve", u_, u_, q_, ALU.mult, ["TMPV"], ["TMPV"])
            ts("dve", DV[:, db + 16:db + 24], u_, -4.0, ALU.mult, ["TMPV"], ["DV"])
            ts("dve", DV[:, db + 24:db + 32], VEC[:, vb + 40:vb + 48], 0.5, ALU.mult, ["VEC"], ["DV"])
            ts("dve", DV[:, db + 32:db + 40], VEC[:, vb + 48:vb + 56], 0.5, ALU.mult, ["VEC"], ["DV"])

        HC = CH // 2

        P.add("pool", lambda e: e.memset(CHS[:].rearrange("p a b c -> p (a b c)"), 0.0), [], ["CHS"])
        P.add("pool", lambda e: e.memset(RHS[:].rearrange("p a b c -> p (a b c)"), 0.0), [], ["RHS"])
        P.add("pool", lambda e: e.memset(HST[:].rearrange("p a b -> p (a b)"), 0.0), [], [f"HST{l}_{ft}" for l in range(DEPTH) for ft in range(8)])

        stream = []
        order = [0, 1, 15, 2, 3, 4, 23, 24, 25, 26, 5, 6, 7, 8, 27, 16, 9, 17, 10, 11, 18, 12, 13, 19, 14, 20, 21, 22]
        for it in range(n_tiles):
            for l in range(depth):
                for c in order:
                    stream.append((l, c))
        sstate = {"issued": 0, "next": 0}

        pending_wb = []
        cast_rot = [0]
        n_first = depth * len(order)

        def flush_wb():
            for (l, c, s, used) in pending_wb:
                dma(wsc[l, c, :, 0:used], RNG[:, s, 0:used], f"wb{s}", [f"RNG{s}"], [f"WSC{l}_{c}"], eng="act")
            pending_wb.clear()

        staged = {}

        def stage_in(i):
            if i in staged or i >= n_first:
                return
            l, c = stream[i]
            lst = []
            if c < NPK:
                used = SC_USED[c]
                for hh in range(2):
                    lo = hh * HC
                    if lo >= used:
                        continue
                    w = min(HC, used - lo)
                    q = cast_rot[0] % NSTG
                    cast_rot[0] += 1
                    dma(STG[:, q, 0:w], wpk[l, c, :, lo:lo + w], f"sf{q}", [], [f"STG{q}"])
                    lst.append((q, lo, w))
            staged[i] = lst

        def fill_from_fp32(i, l, c, s):
            used = SC_USED[c]
            vb = l * NVL
            stage_in(i)
            if c < NPK:
                for (q, lo, w) in staged[i]:
                    act(RNG[:, s, lo:lo + w], STG[:, q, 0:w], AF.Copy, [f"STG{q}"], [f"RNG{s}"])
            elif c < 27:
                ct = c - 23
                for k in range(31):
                    col = vb + 64 + ct * 31 + k
                    ts("dve", RNG[:, s, k * 128:(k + 1) * 128], IDENT, VEC[:, col:col + 1],
                       ALU.mult, ["CST", "VEC"], [f"RNG{s}"])
            else:
                for j in range(32):
                    col = vb + 188 + j
                    ts("dve", RNG[:, s, j * 128:(j + 1) * 128], IDENT, VEC[:, col:col + 1],
                       ALU.mult, ["CST", "VEC"], [f"RNG{s}"])
            pending_wb.append((l, c, s, used))
            nxt = i + 1
            while nxt < n_first and stream[nxt][1] >= NPK:
                nxt += 1
            stage_in(nxt)

        def issue_loads(upto):
            while sstate["issued"] < min(upto, len(stream)):
                i = sstate["issued"]
                l, c = stream[i]
                s = i % RING
                used = SC_USED[c]
                flush_wb()
                if i < n_first:
                    fill_from_fp32(i, l, c, s)
                else:
                    dma(RNG[:, s, 0:used], wsc[l, c, :, 0:used], f"ring{s}", [f"WSC{l}_{c}"], [f"RNG{s}"])
                sstate["issued"] += 1

        def next_chunk(expect):
            i = sstate["next"]
            assert stream[i][1] == expect, (stream[i], expect)
            issue_loads(i + RING - 1)
            sstate["next"] += 1
            s = i % RING
            return RNG[:, s, :], f"RNG{s}"

        def sumsq_rstd(src_list, n_feat_scale_eps):
            ps, pk = pbank()
            for i, (a, k) in enumerate(src_list):
                mm(ps, ONES[:], a, i == 0, i == len(src_list) - 1, ["ONES", k], [pk])
            return ps, pk

        for it in range(n_tiles):
            tok0 = it * T
            dma(X[:], xT[:, :, tok0:tok0 + T].rearrange("k p t -> p k t"), "x", [], [f"X{k}" for k in range(NKT)])
            for l in range(depth):
                vb = l * NVL
                db = l * 40

                def vcol(off, i=0):
                    return VEC[:, vb + off + i:vb + off + i + 1]

                def dcol(off, i=0):
                    return DV[:, db + off + i:db + off + i + 1]

                sq = []
                for kt in range(NKT):
                    act(SQR[:, kt, :], X[:, kt, :], AF.Square, [f"X{kt}"], [f"SQR{kt}"])
                    sq.append((SQR[:, kt, :], f"SQR{kt}"))
                ps, pk = sumsq_rstd(sq, None)
                act(RSTD[:], ps, AF.Ln, [pk, "EPSD"], ["RSTD"], bias=EPSD[:, 0:1])
                act(RSTD[:], RSTD[:], AF.Exp, ["RSTD"], ["RSTD"], scale=-0.5)
                for kt in range(NKT):
                    stt(H[:, kt, :], X[:, kt, :], dcol(0, kt), RSTD[:], ALU.mult, ALU.mult,
                        [f"X{kt}", "DV", "RSTD"], [f"H{kt}"])
                Hk = [f"H{kt}" for kt in range(NKT)]
                DBG = debug and it == debug - 1 and l == 0
                if DBG:
                    dump("H", H[:], Hk, BF16)
                    dump("RSTD", RSTD[:], ["RSTD"], F32)

                w0, w0k = next_chunk(0)
                w0v = w0.rearrange("p (k f) -> p k f", k=NKT)
                ptok = []
                for t4 in range(4):
                    ps, pk = pbank()
                    for kt in range(NKT):
                        mm(ps, H[:, kt, t4 * 128:(t4 + 1) * 128], w0v[:, kt, :], kt == 0, kt == NKT - 1,
                           [Hk[kt], w0k], [pk])
                    b, bk = btile()
                    act(b, ps, AF.Copy, [pk], [bk])
                    ptok.append((b, bk))
                w1, w1k = next_chunk(1)
                w1v = w1.rearrange("p (k f) -> p k f", k=NKT)
                sg = []
                for ft in range(4):
                    ps, pk = pbank()
                    for kt in range(NKT):
                        mm(ps, w1v[:, kt, ft * 128:(ft + 1) * 128], H[:, kt, :], kt == 0, kt == NKT - 1,
                           [Hk[kt], w1k], [pk])
                    b, bk = btile()
                    act(b, ps, AF.Silu, [pk], [bk])
                    sg.append((b, bk))
                pw, pwk = next_chunk(15)
                pwv = pw[:, 0:512].rearrange("p (g f) -> p g f", g=4)
                for g in range(4):
                    ps, pk = pbank()
                    for t4 in range(4):
                        first = (it == 0 and t4 == 0)
                        bidx = (8 + g) if first else g
                        cur, curk = ptok[t4]
                        mm(ps[:, t4 * 128:(t4 + 1) * 128], cur[:, g * 128:(g + 1) * 128], BAND[:, bidx, :],
                           True, first, [curk, "BAND"], [pk])
                        if not first:
                            if t4 == 0:
                                prv, prvk = PTOKS[:, l, :], f"PTOKS{l}"
                            else:
                                prv, prvk = ptok[t4 - 1]
                            mm(ps[:, t4 * 128:(t4 + 1) * 128], prv[:, g * 128:(g + 1) * 128], BAND[:, 4 + g, :],
                               False, True, [prvk, "BAND"], [pk])
                    pl, plk = btile()
                    act(pl, ps, AF.Copy, [pk], [plk])
                    ps2, pk2 = pbank()
                    mm(ps2, pwv[:, g, :], pl, True, True, [plk, pwk], [pk2])
                    stt(UP[:, g, :], ps2, vcol(16, g), sg[g][0], ALU.mult, ALU.mult, [pk2, "VEC", sg[g][1]], [f"UP{g}"])
                cp("pool", PTOKS[:, l, :], ptok[3][0], [ptok[3][1]], [f"PTOKS{l}"])
                if DBG:
                    dump("UP", UP[:], [f"UP{g}" for g in range(4)], BF16)

                cp("pool", C[:, :, 0:30], CHS[:, l, :, 0:30], ["CHS"], [f"C{ft}" for ft in range(4)])
                w2, w2k = next_chunk(2)
                w3, w3k = next_chunk(3)
                w2v = w2.rearrange("p (k f) -> p k f", k=NKT)
                w3v = w3.rearrange("p (k f) -> p k f", k=NKT)
                for ft in range(4):
                    psv, pkv = pbank()
                    for kt in range(NKT):
                        mm(psv, w2v[:, kt, ft * 128:(ft + 1) * 128], H[:, kt, :], kt == 0, kt == NKT - 1,
                           [Hk[kt], w2k], [pkv])
                    psg, pkg = pbank()
                    for kt in range(NKT):
                        mm(psg, w3v[:, kt, ft * 128:(ft + 1) * 128], H[:, kt, :], kt == 0, kt == NKT - 1,
                           [Hk[kt], w3k], [pkg])
                    f, fk = ftile()
                    act(f, psg, AF.Sigmoid, [pkg], [fk])
                    if DBG and ft == 3:
                        dump("SIG3", f, [fk], F32)
                        f2_, f2k_ = ftile()
                        cp("dve", f2_, psv, [pkv], [f2k_])
                        dump("CV3", f2_, [f2k_], F32)
                        f3_, f3k_ = ftile()
                        tt("dve", f3_, psv, f, ALU.mult, [pkv, fk], [f3k_])
                        dump("CM3", f3_, [f3k_], F32)
                    tt("dve", C[:, ft, 30:30 + T], psv, f, ALU.mult, [pkv, fk], [f"C{ft}"])
                cp("pool", CHS[:, l, :, 0:30], C[:, :, T:T + 30], [f"C{ft}" for ft in range(4)], ["CHS"])
                w4, w4k = next_chunk(4)
                w4v = w4.rearrange("p (k f) -> p k f", k=NKT)
                scg = []
                for ft in range(4):
                    ps, pk = pbank()
                    for kt in range(NKT):
                        mm(ps, w4v[:, kt, ft * 128:(ft + 1) * 128], H[:, kt, :], kt == 0, kt == NKT - 1,
                           [Hk[kt], w4k], [pk])
                    b, bk = btile()
                    act(b, ps, AF.Silu, [pk], [bk])
                    scg.append((b, bk))
                cc = []
                csq = []
                for ft in range(4):
                    acc, acck = M[:, ft, :], f"M{ft}"
                    for k in range(NDT):
                        wcol = vcol(64 + ft * 31 + k)
                        if k == 0:
                            ts("dve", acc, C[:, ft, k:k + T], wcol, ALU.mult, [f"C{ft}", "VEC"], [acck])
                        else:
                            stt(acc, C[:, ft, k:k + T], wcol, acc, ALU.mult, ALU.add, [f"C{ft}", "VEC", acck], [acck])
                for ft in range(4):
                    dd, ddk = next_chunk(23 + ft)
                    ddv = dd[:, 0:31 * 128].rearrange("p (k f) -> p k f", k=31)
                    ps, pk = pbank()
                    for k in range(NDT, 31):
                        mm(ps, ddv[:, k, :], C[:, ft, k:k + T], k == NDT, k == 30, [f"C{ft}", ddk], [pk])
                    acc, acck = M[:, ft, :], f"M{ft}"
                    a, ak = SQR[:, ft, :], f"SQR{ft}"
                    stt(a, ps, vcol(20, ft), acc, ALU.add, ALU.add, [pk, "VEC", acck], [ak])
                    a2, a2k = SQR[:, 4 + ft, :], f"SQR{4 + ft}"
                    act(a2, a.bitcast(F32), AF.Square, [ak], [a2k])
                    cc.append((a, ak))
                    csq.append((a2, a2k))
                psm, pkm = sumsq_rstd(cc, None)
                pss, pks = sumsq_rstd(csq, None)
                mean, meank = ftile()
                ts("dve", mean, psm, 1.0 / 512, ALU.mult, [pkm], [meank])
                m2, m2k = ftile()
                tt("dve", m2, mean, mean, ALU.mult, [meank], [m2k])
                var, vark = ftile()
                stt(var, pss, 1.0 / 512, m2, ALU.mult, ALU.subtract, [pks, m2k], [vark])
                act(var, var, AF.Ln, [vark, "EPSD"], [vark], bias=EPSD[:, 1:2])
                act(var, var, AF.Exp, [vark], [vark], scale=-0.5)
                for ft in range(4):
                    a0, a0k = cc[ft]
                    a, ak = ftile()
                    tt("dve", a, a0.bitcast(F32), mean, ALU.subtract, [a0k, meank], [ak])
                    tt("dve", a, a, var, ALU.mult, [ak, vark], [ak])
                    b, bk = btile()
                    act(b, a, AF.Silu, [ak, "VEC"], [bk], bias=vcol(28, ft), scale=vcol(24, ft))
                    tt("pool", UC[:, ft, :], b, scg[ft][0], ALU.mult, [bk, scg[ft][1]], [f"UC{ft}"])

                if DBG:
                    dump("UC", UC[:], [f"UC{g}" for g in range(4)], BF16)
                    dump("C", C[:], [f"C{g}" for g in range(4)], BF16)
                cp("pool", R[:, :, 0:3], RHS[:, l, :, 0:3], ["RHS"], [f"R{ft}" for ft in range(8)])
                for half in range(2):
                    w5, w5k = next_chunk(5 + half)
                    w5v = w5.rearrange("p (k f) -> p k f", k=NKT)
                    for f4 in range(4):
                        ft = half * 4 + f4
                        ps, pk = pbank()
                        for kt in range(NKT):
                            mm(ps, w5v[:, kt, f4 * 128:(f4 + 1) * 128], H[:, kt, :], kt == 0, kt == NKT - 1,
                               [Hk[kt], w5k], [pk])
                        act(R[:, ft, 3:3 + T], ps, AF.Copy, [pk], [f"R{ft}"])
                cp("pool", RHS[:, l, :, 0:3], R[:, :, T:T + 3], [f"R{ft}" for ft in range(8)], ["RHS"])
                srg = []
                for half in range(2):
                    w7, w7k = next_chunk(7 + half)
                    w7v = w7.rearrange("p (k f) -> p k f", k=NKT)
                    for f4 in range(4):
                        ps, pk = pbank()
                        for kt in range(NKT):
                            mm(ps, w7v[:, kt, f4 * 128:(f4 + 1) * 128], H[:, kt, :], kt == 0, kt == NKT - 1,
                               [Hk[kt], w7k], [pk])
                        b, bk = btile()
                        act(b, ps, AF.Silu, [pk], [bk])
                        srg.append((b, bk))
                d4, d4k = next_chunk(27)
                d4v = d4.rearrange("p (f k j) -> p f k j", f=8, k=4)
                wax, waxk = next_chunk(16)
                wav = wax[:, 0:2048].rearrange("p (h k f) -> p h k f", h=4, k=2)
                wxv = wax[:, 2048:4096].rearrange("p (h k f) -> p h k f", h=4, k=2)
                xc32_h = {}
                xcb_h = {}

                def lfp(hd, k):
                    i = (hd % 2) * 8 + k
                    return FP[:, i, :], f"FP{i}"

                def lru_s1(hd):
                    xc32 = []
                    xcb = []
                    for f2 in range(2):
                        ft = hd * 2 + f2
                        ps, pk = pbank()
                        for k in range(4):
                            mm(ps, d4v[:, ft, k, :], R[:, ft, k:k + T], k == 0, k == 3, [f"R{ft}", d4k], [pk])
                        a, ak = lfp(hd, f2)
                        ts("dve", a, ps, vcol(32, ft), ALU.add, [pk, "VEC"], [ak])
                        b, bk = btile()
                        cp("dve", b, a, [ak], [bk])
                        xc32.append((a, ak))
                        xcb.append((b, bk))
                    xc32_h[hd] = xc32
                    xcb_h[hd] = xcb

                rg_h = {}
                ig_h = {}

                def lru_s2a(hd):
                    xcb = xcb_h[hd]
                    rg = []
                    ig = []
                    for f2 in range(2):
                        ft = hd * 2 + f2
                        psr, pkr = pbank()
                        for k2 in range(2):
                            mm(psr, wav[:, hd, k2, f2 * 128:(f2 + 1) * 128], xcb[k2][0], k2 == 0, k2 == 1,
                               [xcb[k2][1], waxk], [pkr])
                        psi, pki = pbank()
                        for k2 in range(2):
                            mm(psi, wxv[:, hd, k2, f2 * 128:(f2 + 1) * 128], xcb[k2][0], k2 == 0, k2 == 1,
                               [xcb[k2][1], waxk], [pki])
                        a, ak = lfp(hd, 2 + f2)
                        act(a, psr, AF.Tanh, [pkr, "DV"], [ak], bias=dcol(24, ft), scale=0.5)
                        b, bk = lfp(hd, 4 + f2)
                        act(b, psi, AF.Tanh, [pki, "DV"], [bk], bias=dcol(32, ft), scale=0.5)
                        rg.append((a, ak))
                        ig.append((b, bk))
                    rg_h[hd] = rg
                    ig_h[hd] = ig

                def lru_s2b(hd):
                    xc32 = xc32_h[hd]
                    rg = rg_h[hd]
                    ig = ig_h[hd]
                    avs = []
                    for f2 in range(2):
                        ft = hd * 2 + f2
                        av, avk = lfp(hd, 6 + f2)
                        act(av, rg[f2][0], AF.Exp, [rg[f2][1], "DV"], [avk], scale=dcol(16, ft), bias=dcol(16, ft))
                        a2, a2k = rg[f2]
                        tt("dve", a2, av, av, ALU.mult, [avk], [a2k])
                        avs.append((av, avk))
                    for f2 in range(2):
                        ft = hd * 2 + f2
                        av, avk = avs[f2]
                        a2, a2k = rg[f2]
                        act(a2, a2, AF.Sqrt, [a2k, "EPSD"], [a2k], bias=EPSD[:, 2:3], scale=-1.0)
                        bt, btk = ig[f2]
                        stt(bt, bt, 1.0, a2, ALU.add, ALU.mult, [btk, a2k], [btk])
                        stt(bt, bt, 0.5, xc32[f2][0], ALU.mult, ALU.mult, [btk, xc32[f2][1]], [btk])
                        hs, hsk = a2, a2k
                        P.add("dve", lambda e, av=av, bt=bt, hs=hs, ini=HST[:, l, ft:ft + 1]: e.tensor_tensor_scan(
                            hs, av, bt, ini, ALU.mult, ALU.add), [avk, btk, f"HST{l}_{ft}"], [hsk])
                        cp("pool", HST[:, l, ft:ft + 1], hs[:, T - 1:T], [hsk], [f"HST{l}_{ft}"])
                        tt("pool", UL[:, ft, :], hs, srg[ft][0], ALU.mult, [hsk, srg[ft][1]], [f"UL{ft}"])

                lru_s1(0)
                lru_s1(1)
                lru_s2a(0)
                lru_s2b(0)
                lru_s1(2)
                lru_s2a(1)
                lru_s2b(1)
                lru_s1(3)
                lru_s2a(2)
                lru_s2a(3)

                if DBG:
                    dump("UL", UL[:], [f"UL{g}" for g in range(8)], BF16)
                    dump("DV", DV[:], ["DV"], F32)
                branches = [
                    (17, (9, 10), None, UP, 4, "UP"),
                    (18, (11, 12), None, UC, 4, "UC"),
                    ((19, 20), (13, 14), None, UL, 8, "UL"),
                ]
                for bi, (wy, wg, _, U, nk, uname) in enumerate(branches):
                    for half in range(2):
                        wgc, wgck = next_chunk(wg[half])
                        wgv = wgc.rearrange("p (k f) -> p k f", k=NKT)
                        if bi < 2 and half == 0:
                            wyc, wyck = next_chunk(wy)
                            wyv = wyc.rearrange("p (k f) -> p k f", k=nk)
                        if bi == 2:
                            wyc, wyck = next_chunk(wy[half])
                            wyv = wyc.rearrange("p (k f) -> p k f", k=nk)
                        for j4 in range(4):
                            j = half * 4 + j4
                            psg, pkg = pbank()
                            for kt in range(NKT):
                                mm(psg, wgv[:, kt, j4 * 128:(j4 + 1) * 128], H[:, kt, :], kt == 0, kt == NKT - 1,
                                   [Hk[kt], wgck], [pkg])
                            psy, pky = pbank()
                            ycol = j4 * 128 if bi == 2 else j * 128
                            for kt in range(nk):
                                mm(psy, wyv[:, kt, ycol:ycol + 128], U[:, kt, :], kt == 0, kt == nk - 1,
                                   [f"{uname}{kt}", wyck], [pky])
                            ci = ccnt[0] % 2
                            ccnt[0] += 1
                            f, fk = CF[:, ci, :], f"CF{ci}"
                            act(f, psg, AF.Sigmoid, [pkg], [fk])
                            if bi == 0:
                                tt("dve", M[:, j, :], psy, f, ALU.mult, [pky, fk], [f"M{j}"])
                            else:
                                tt("dve", f, psy, f, ALU.mult, [pky, fk], [fk])
                                if bi == 1:
                                    tt("pool", M[:, j, :], M[:, j, :], f, ALU.add, [f"M{j}", fk], [f"M{j}"])
                                else:
                                    tt("pool", MB[:, j, :], M[:, j, :], f, ALU.add, [f"M{j}", fk], [f"MB{j}"])
                            if bi == 0 and j == 1:
                                lru_s2b(2)
                            if bi == 0 and j == 5:
                                lru_s2b(3)

                if DBG:
                    dump("MB", MB[:], [f"MB{g}" for g in range(8)], BF16)
                sq = []
                for half in range(2):
                    wo, wok = next_chunk(21 + half)
                    wov = wo.rearrange("p (k f) -> p k f", k=NKT)
                    for j4 in range(4):
                        j = half * 4 + j4
                        ps, pk = pbank()
                        for kt in range(NKT):
                            mm(ps, wov[:, kt, j4 * 128:(j4 + 1) * 128], MB[:, kt, :], kt == 0, kt == NKT - 1,
                               [f"MB{kt}", wok], [pk])
                        act(M[:, j, :], ps, AF.Copy, [pk], [f"M{j}"])
                        act(SQR[:, j, :], ps, AF.Square, [pk], [f"SQR{j}"])
                        sq.append((SQR[:, j, :], f"SQR{j}"))
                ps, pk = sumsq_rstd(sq, None)
                act(RSTD[:], ps, AF.Ln, [pk, "EPSD"], ["RSTD"], bias=EPSD[:, 0:1])
                act(RSTD[:], RSTD[:], AF.Exp, ["RSTD"], ["RSTD"], scale=-0.5)
                for j in range(NKT):
                    stt(M[:, j, :], M[:, j, :], dcol(8, j), RSTD[:], ALU.mult, ALU.mult, [f"M{j}", "DV", "RSTD"], [f"M{j}"])
                    tt("dve" if j % 2 == 0 else "pool", X[:, j, :], X[:, j, :], M[:, j, :], ALU.add,
                       [f"X{j}", f"M{j}"], [f"X{j}"])
            dma(oT[:, :, tok0:tok0 + T].rearrange("k p t -> p k t"), X[:], "o", [f"X{k}" for k in range(NKT)], [])

        assert sstate["next"] == len(stream)
        flush_wb()
        P.emit(nc, sems, (["o", "dbg"] if debug else ["o"]) + [f"wb{i}" for i in range(RING) if f"wb{i}" in P.dma_cnt])
    return nc


def _band_consts():
    cst = np.zeros((128, NCONST), np.float32)
    cst[:, 0:128] = np.eye(128, dtype=np.float32)
    cst[:, 128:256] = 1.0
    tp = np.arange(128)[:, None]
    t = np.arange(128)[None, :]
    for g, w in enumerate((2, 4, 8, 16)):
        lag = t - tp
        b0 = np.where((lag >= 0) & (lag <= w - 1), 1.0 / w, 0.0) - (lag == 0)
        lag1 = t + 128 - tp
        b1 = np.where((lag1 >= 1) & (lag1 <= w - 1), 1.0 / w, 0.0)
        cnt = np.minimum(t + 1, w).astype(np.float64)
        b0f = np.where((lag >= 0) & (lag <= w - 1), 1.0 / cnt, 0.0) - (lag == 0)
        cst[:, 256 + g * 128:256 + (g + 1) * 128] = b0
        cst[:, 768 + g * 128:768 + (g + 1) * 128] = b1
        cst[:, 1280 + g * 128:1280 + (g + 1) * 128] = b0f
    return cst


def _pack(inputs, depth):
    f = lambda k: np.asarray(inputs[k], dtype=np.float32)
    w_in = f("w_in")
    wpk = np.zeros((depth, NPK, 128, CH), np.float32)
    vecs = np.zeros((128, DEPTH * NVL), np.float32)

    def kmaj(w):
        K, N = w.shape
        return w.reshape(K // 128, 128, N).transpose(1, 0, 2).reshape(128, -1)

    def colv(v):
        return v.reshape(-1, 128).T

    for l in range(depth):
        for c in range(15):
            wpk[l, c] = kmaj(w_in[l][:, c * 512:(c + 1) * 512])
        wpk[l, 15, :, 0:512] = f("pool_w")[l].transpose(1, 0, 2).reshape(128, 512)
        wa = f("lru_wa")[l].reshape(4, 2, 128, 256).transpose(2, 0, 1, 3).reshape(128, 2048)
        wx = f("lru_wx")[l].reshape(4, 2, 128, 256).transpose(2, 0, 1, 3).reshape(128, 2048)
        wpk[l, 16, :, 0:2048] = wa
        wpk[l, 16, :, 2048:4096] = wx
        wpk[l, 17] = kmaj(f("w_pool_out")[l])
        wpk[l, 18] = kmaj(f("w_conv_out")[l])
        wpk[l, 19] = kmaj(f("w_lru_out")[l][:, 0:512])
        wpk[l, 20] = kmaj(f("w_lru_out")[l][:, 512:1024])
        wpk[l, 21] = kmaj(f("w_out")[l][:, 0:512])
        wpk[l, 22] = kmaj(f("w_out")[l][:, 512:1024])
        vb = l * NVL
        vecs[:, vb + 0:vb + 8] = colv(f("norm_pre")[l])
        vecs[:, vb + 8:vb + 16] = colv(f("norm_post")[l])
        vecs[:, vb + 16:vb + 20] = colv(f("pool_scale")[l])
        vecs[:, vb + 20:vb + 24] = colv(f("conv_b")[l])
        vecs[:, vb + 24:vb + 28] = colv(f("conv_ln_g")[l])
        vecs[:, vb + 28:vb + 32] = colv(f("conv_ln_b")[l])
        vecs[:, vb + 32:vb + 40] = colv(f("lru_conv_b")[l])
        vecs[:, vb + 40:vb + 48] = colv(f("lru_ba")[l])
        vecs[:, vb + 48:vb + 56] = colv(f("lru_bx")[l])
        vecs[:, vb + 56:vb + 64] = colv(f("lru_lambda")[l])
        cw = f("conv_dw")[l]
        vecs[:, vb + 64:vb + 188] = cw.reshape(31, 4, 128).transpose(2, 1, 0).reshape(128, 124)
        lw = f("lru_conv_w")[l]
        vecs[:, vb + 188:vb + 220] = lw.reshape(4, 8, 128).transpose(2, 1, 0).reshape(128, 32)
    return wpk, vecs


def run(inputs, n_tiles=SEQ // T, depth=DEPTH, n_cores=8, trace=False, debug=False):
    x = np.asarray(inputs["x"], dtype=np.float32)
    ntok = n_tiles * T
    wpk, vecs = _pack(inputs, depth)
    cst = _band_consts()
    nc = build_nc(n_tiles, depth, debug)
    in_maps = []
    for b in range(n_cores):
        xT = np.ascontiguousarray(x[b, :ntok, :].T).reshape(NKT, 128, ntok)
        in_maps.append({"xT": xT, "wpk": wpk, "vecs": vecs, "cst": cst})
    res = run_bass_kernel_spmd(nc, in_maps, core_ids=list(range(n_cores)), trace=trace)
    out = np.stack([r["oT"].reshape(D, ntok).T for r in res.results], axis=0)
    return out, res


def kernel(**inputs):
    out, _ = run(inputs)
    return np.ascontiguousarray(out.astype(np.float32))
```
